# Optimizing a Trainium2 kernel written in Bass

```python
import math
import jax, jax.numpy as jnp
from jax import lax
import numpy as np

D_MODEL = 1024
BATCH = 8
SEQ = 4096
DEPTH = 2

D_G = D_MODEL // 4
N_GROUPS = 5
D_MIX = N_GROUPS * D_G
N_SUB = 4
HEAD_DIM = D_G // N_SUB
CONV_A = 3
CONV_D = 31
CHUNK = 128
POOL_WINDOWS = (2, 4, 8, 16)
MEM_LEN = 256
LN_EPS = 1e-5
DEEPNORM_ALPHA = (2.0 * DEPTH) ** 0.25
DEEPNORM_BETA = (8.0 * DEPTH) ** -0.25

SPLIT_SIZES = (D_G, D_G, D_G, D_G, D_G, D_G, D_G, D_G, D_G, D_MIX)
D_IN = sum(SPLIT_SIZES)
SPLIT_OFFSETS = tuple(int(o) for o in np.cumsum(SPLIT_SIZES)[:-1])

kernel_name = "hybrid_parallel_conv_sgu_pool_conformer_memxattn"


def layer_norm(x, g, b):
    xf = x.astype(jnp.float32)
    mu = jnp.mean(xf, axis=-1, keepdims=True)
    var = jnp.mean(jnp.square(xf - mu), axis=-1, keepdims=True)
    y = (xf - mu) * lax.rsqrt(var + LN_EPS) * g.astype(jnp.float32) + b.astype(jnp.float32)
    return y.astype(x.dtype)


def causal_depthwise_conv(x, w):
    k, c = w.shape
    return lax.conv_general_dilated(
        x, w[:, None, :].astype(x.dtype), window_strides=(1,), padding=[(k - 1, 0)],
        dimension_numbers=("NWC", "WIO", "NWC"), feature_group_count=c)


def short_gated_conv(xa, ba, ca, w_conv):
    return ba * causal_depthwise_conv(ca * xa, w_conv)


def spatial_gating(u, v, ln_g, ln_b, w_s, b_s):
    bn, s, _ = v.shape
    u = jax.nn.gelu(u)
    v = layer_norm(jax.nn.gelu(v), ln_g, ln_b)
    v = v.reshape(bn, s // CHUNK, CHUNK, N_SUB, HEAD_DIM)
    mask = jnp.tril(jnp.ones((CHUNK, CHUNK), dtype=bool))
    w = jnp.where(mask[None], w_s, jnp.zeros_like(w_s))
    mixed = jnp.einsum("hts,bcshd->bcthd", w, v) + b_s.T[:, :, None]
    return u * mixed.reshape(bn, s, D_G)


def multiscale_pool(xc, w_grp, scale):
    bn, s, _ = xc.shape
    xf = xc.astype(jnp.float32).reshape(bn, s, N_SUB, HEAD_DIM)
    cs = jnp.pad(jnp.cumsum(xf, axis=1), ((0, 0), (1, 0), (0, 0), (0, 0)))
    t = jnp.arange(s)
    win = jnp.array(POOL_WINDOWS, dtype=jnp.int32)
    lo = jnp.maximum(t[:, None] + 1 - win[None, :], 0)
    window_sum = cs[:, 1:] - cs[:, lo, jnp.arange(N_SUB)[None, :]]
    count = jnp.minimum(t[:, None] + 1, win[None, :]).astype(jnp.float32)
    y = (window_sum / count[None, :, :, None] - xf).astype(xc.dtype)
    y = jnp.einsum("bsgc,gcd->bsgd", y, w_grp)
    return y.reshape(bn, s, D_G) * scale


def conformer_conv(a, g, w_dw, b_dw, ln_g, ln_b, w_pw):
    h = a * jax.nn.sigmoid(g)
    h = causal_depthwise_conv(h, w_dw) + b_dw
    h = jax.nn.silu(layer_norm(h, ln_g, ln_b))
    return h @ w_pw


def memory_cross_attention(q, mem, w_kv):
    bn, s, _ = q.shape
    k, v = jnp.split(mem @ w_kv, 2, axis=-1)
    q = q.reshape(bn, s, N_SUB, HEAD_DIM)
    k = k.reshape(bn, -1, N_SUB, HEAD_DIM)
    v = v.reshape(bn, -1, N_SUB, HEAD_DIM)
    scores = jnp.einsum("bshd,bmhd->bhsm", q, k).astype(jnp.float32) * (1.0 / math.sqrt(HEAD_DIM))
    p = jax.nn.softmax(scores, axis=-1).astype(v.dtype)
    return jnp.einsum("bhsm,bmhd->bshd", p, v).reshape(bn, s, D_G)


def hybrid_mixer(x, mem, w_in, conv_a_w, sg_ln_g, sg_ln_b, sg_w, sg_b, pool_w, pool_scale,
                 cc_dw_w, cc_dw_b, cc_ln_g, cc_ln_b, cc_pw_w, w_kv, w_out):
    proj = x @ w_in
    xa, ba, ca, u, v, xc, da, dg, q, gate = jnp.split(proj, SPLIT_OFFSETS, axis=-1)
    y_a = short_gated_conv(xa, ba, ca, conv_a_w)
    y_b = spatial_gating(u, v, sg_ln_g, sg_ln_b, sg_w, sg_b)
    y_c = multiscale_pool(xc, pool_w, pool_scale)
    y_d = conformer_conv(da, dg, cc_dw_w, cc_dw_b, cc_ln_g, cc_ln_b, cc_pw_w)
    y_e = memory_cross_attention(q, mem, w_kv)
    h = jnp.concatenate([y_a, y_b, y_c, y_d, y_e], axis=-1) * jax.nn.silu(gate)
    return h @ w_out


def setup_inputs(seed: int = 0) -> dict:
    key = jax.random.key(seed)
    ks = jax.random.split(key, 20)
    L = DEPTH

    def nrm(k, shape, scale):
        return jax.random.normal(k, shape, jnp.float32) * scale

    return {
        "x": nrm(ks[0], (BATCH, SEQ, D_MODEL), 1.0),
        "mem": nrm(ks[1], (BATCH, MEM_LEN, D_MODEL), 1.0),
        "w_in": nrm(ks[2], (L, D_MODEL, D_IN), D_MODEL ** -0.5),
        "conv_a_w": nrm(ks[3], (L, CONV_A, D_G), CONV_A ** -0.5),
        "sg_ln_g": 1.0 + nrm(ks[4], (L, D_G), 0.05),
        "sg_ln_b": nrm(ks[5], (L, D_G), 0.05),
        "sg_w": nrm(ks[6], (L, N_SUB, CHUNK, CHUNK), CHUNK ** -0.5),
        "sg_b": 1.0 + nrm(ks[7], (L, N_SUB, CHUNK), 0.1),
        "pool_w": nrm(ks[8], (L, N_SUB, HEAD_DIM, HEAD_DIM), HEAD_DIM ** -0.5),
        "pool_scale": 1.0 + nrm(ks[9], (L, D_G), 0.1),
        "cc_dw_w": nrm(ks[10], (L, CONV_D, D_G), CONV_D ** -0.5),
        "cc_dw_b": nrm(ks[11], (L, D_G), 0.02),
        "cc_ln_g": 1.0 + nrm(ks[12], (L, D_G), 0.05),
        "cc_ln_b": nrm(ks[13], (L, D_G), 0.05),
        "cc_pw_w": nrm(ks[14], (L, D_G, D_G), D_G ** -0.5),
        "w_kv": nrm(ks[15], (L, D_MODEL, 2 * D_G), D_MODEL ** -0.5),
        "w_out": nrm(ks[16], (L, D_MIX, D_MODEL), D_MIX ** -0.5 * DEEPNORM_BETA),
        "ln_g": 1.0 + nrm(ks[17], (L, D_MODEL), 0.05),
        "ln_b": nrm(ks[18], (L, D_MODEL), 0.05),
    }


def reference(x, mem, w_in, conv_a_w, sg_ln_g, sg_ln_b, sg_w, sg_b, pool_w, pool_scale,
              cc_dw_w, cc_dw_b, cc_ln_g, cc_ln_b, cc_pw_w, w_kv, w_out, ln_g, ln_b):
    for l in range(DEPTH):
        y = hybrid_mixer(x, mem, w_in[l], conv_a_w[l], sg_ln_g[l], sg_ln_b[l], sg_w[l], sg_b[l],
                         pool_w[l], pool_scale[l], cc_dw_w[l], cc_dw_b[l], cc_ln_g[l], cc_ln_b[l],
                         cc_pw_w[l], w_kv[l], w_out[l])
        x = layer_norm(DEEPNORM_ALPHA * x + y, ln_g[l], ln_b[l])
    return x
```

```python
import numpy as np
from contextlib import ExitStack
import concourse.bass as bass
import concourse.mybir as mybir
from concourse.bass_utils import run_bass_kernel_spmd

F32, BF16 = mybir.dt.float32, mybir.dt.bfloat16
AF = mybir.ActivationFunctionType
ALU = mybir.AluOpType

D = 1024; S = 4096; DG = 256; DIN = 3584; DMIX = 1280; MEM = 256
T = 512; NT = 4; NBLK = S // T
ALPHA = float((2.0 * 2) ** 0.25)
EPS = 1e-5
N_CORES = 8
PARAMS = ["w_in", "conv_a_w", "sg_ln_g", "sg_ln_b", "sg_w", "sg_b", "pool_w", "pool_scale",
          "cc_dw_w", "cc_dw_b", "cc_ln_g", "cc_ln_b", "cc_pw_w", "w_kv", "w_out", "ln_g", "ln_b"]
PSHAPES = {"w_in": [D, DIN], "conv_a_w": [3, DG], "sg_ln_g": [DG], "sg_ln_b": [DG], "sg_w": [4, 128, 128],
           "sg_b": [4, 128], "pool_w": [4, 64, 64], "pool_scale": [DG], "cc_dw_w": [31, DG], "cc_dw_b": [DG],
           "cc_ln_g": [DG], "cc_ln_b": [DG], "cc_pw_w": [DG, DG], "w_kv": [D, 2 * DG], "w_out": [DMIX, D],
           "ln_g": [D], "ln_b": [D]}


class H:
    __slots__ = ("t", "key", "ring", "slot", "gen")

    def __init__(self, t, key, ring=None, slot=0, gen=0):
        self.t, self.key, self.ring, self.slot, self.gen = t, key, ring, slot, gen


class Ring:
    def __init__(self, name, tensors):
        self.name, self.tensors, self.i = name, tensors, 0
        self.gen = [0] * len(tensors)

    def alloc(self):
        s = self.i % len(self.tensors)
        self.i += 1
        self.gen[s] += 1
        return H(self.tensors[s], (self.name, s), self, s, self.gen[s])


class Tr:
    def __init__(self, nc, es):
        self.nc, self.es = nc, es
        self.engs = {"pe": nc.tensor, "act": nc.scalar, "dve": nc.vector, "pool": nc.gpsimd, "sp": nc.sync}
        self.sem = {e: es.enter_context(nc.semaphore("s_" + e)) for e in self.engs}
        self.cnt = {e: 0 for e in self.engs}
        self.waited = {e: {} for e in self.engs}
        self.lastw = {}
        self.readers = {}
        self.dsem = {}

    def _key(self, b):
        if isinstance(b, H):
            assert b.ring.gen[b.slot] == b.gen, f"ring buffer {b.key} reused while live"
            return b.key
        return b

    def _wait(self, e, tok):
        sem, val, sid = tok
        if self.waited[e].get(sid, 0) >= val:
            return
        self.engs[e].wait_ge(sem, val)
        self.waited[e][sid] = val

    def op(self, e, reads, writes, fn, dma=None):
        rk = [self._key(b) for b in reads]
        wk = [self._key(b) for b in writes]
        deps = []
        for k in rk:
            if k in self.lastw:
                deps.append(self.lastw[k])
        for k in wk:
            if k in self.lastw:
                deps.append(self.lastw[k])
            deps.extend(self.readers.get(k, {}).values())
        for tok in deps:
            if e == "pe" and tok[2] == "pe":
                continue
            self._wait(e, tok)
        inst = fn(self.engs[e])
        if dma is None:
            self.cnt[e] += 1
            inst.then_inc(self.sem[e], 1)
            tok = (self.sem[e], self.cnt[e], e)
        else:
            if dma not in self.dsem:
                self.dsem[dma] = [self.es.enter_context(self.nc.semaphore("d_%d" % len(self.dsem))), 0]
            s = self.dsem[dma]
            s[1] += 16
            inst.then_inc(s[0], 16)
            tok = (s[0], s[1], "dma:" + str(dma))
        for k in rk:
            d = self.readers.setdefault(k, {})
            old = d.get(tok[2])
            if old is None or old[1] < tok[1]:
                d[tok[2]] = tok
        for k in wk:
            self.lastw[k] = tok
            self.readers[k] = {}
        return tok


class StopBuild(Exception):
    pass


STOP = None


def chk(name):
    if STOP == name:
        raise StopBuild()


def build(nl):
    nc = bass.Bass("TRN2", target_bir_lowering=False)
    dr = {}
    dr["x"] = nc.dram_tensor("x", [S, D], F32, kind="ExternalInput").ap()
    dr["mem"] = nc.dram_tensor("mem", [MEM, D], F32, kind="ExternalInput").ap()
    for p in PARAMS:
        dr[p] = nc.dram_tensor(p, [nl] + PSHAPES[p], F32, kind="ExternalInput").ap()
    dr["c_ident"] = nc.dram_tensor("c_ident", [128, 128], F32, kind="ExternalInput").ap()
    dr["c_mask"] = nc.dram_tensor("c_mask", [128, 128], F32, kind="ExternalInput").ap()
    dr["c_invw"] = nc.dram_tensor("c_invw", [128, 2], F32, kind="ExternalInput").ap()
    dr["c_invcnt"] = nc.dram_tensor("c_invcnt", [128, 32], F32, kind="ExternalInput").ap()
    y_out = nc.dram_tensor("y", [S, D], F32, kind="ExternalOutput").ap()
    xscr = [nc.dram_tensor("xscr%d" % i, [S, D], F32, kind="Internal").ap() for i in range(nl - 1)]

    es = ExitStack()
    with es:
        def sb(name, shape, dt):
            return es.enter_context(nc.sbuf_tensor(name, shape, dt))

        tr = Tr(nc, es)
        op = tr.op
        ident_f = sb("ident_f", [128, 128], F32); ident_b = sb("ident_b", [128, 128], BF16)
        mask_f = sb("mask_f", [128, 128], F32)
        invw = sb("invw", [128, 2], F32); invcnt = sb("invcnt", [128, 32], F32)
        ones_b = sb("ones_b", [128, 128], BF16); neghalf = sb("neghalf", [128, 4], F32)
        ones_row = sb("ones_row", [1, 128], BF16)
        w_in_sb = sb("w_in_sb", [128, 8, DIN], BF16)
        w_out_sb = sb("w_out_sb", [128, 10, D], BF16)
        w_pw_sb = sb("w_pw_sb", [128, 2, DG], BF16)
        pool_w_sb = sb("pool_w_sb", [128, 2, 64], BF16)
        PR = sb("PR", [40, DG], F32)
        pcm = sb("pcm", [128, 80], F32)
        bdw_row = sb("bdw_row", [1, DG], BF16)
        lng_bc = sb("lng_bc", [128, D], F32); lnb_bc = sb("lnb_bc", [128, D], F32)
        wT_sb = sb("wT_sb", [128, 512], BF16)
        BSt = sb("BSt", [128, 2, 128], F32)
        BiasB = sb("BiasB", [128, 2, 512], F32)
        diag_sb = sb("diag_sb", [128, 62 * 128], BF16)
        memT = sb("memT", [128, 8, MEM], BF16)
        kT_sb = sb("kT_sb", [128, 2, MEM], BF16)
        v_sb = sb("v_sb", [128, 2, DG], BF16)
        x_bf = sb("x_bf", [128, NT, D], BF16)
        xT = sb("xT", [128, 8, T], BF16)
        hcat = sb("hcat", [128, 10, T], BF16)
        m_ext = [sb("m_ext%d" % i, [128, 2, T + 2], F32) for i in range(2)]
        xc_ext = [sb("xc_ext%d" % i, [128, 2, T + 15], F32) for i in range(2)]
        h_ext = [sb("h_ext%d" % i, [128, 2, T + 30], BF16) for i in range(2)]
        r_buf = [sb("r_buf%d" % i, [128, D], F32) for i in range(2)]
        vn_tm = [sb("vn_tm%d" % i, [128, DG], BF16) for i in range(NT)]
        hn_tm = [sb("hn_tm%d" % i, [128, DG], BF16) for i in range(NT)]
        gelu_u = sb("gelu_u", [128, 2, T], BF16)
        q_sb = sb("q_sb", [128, 2, T], BF16)
        ctmp = sb("ctmp", [128, 16], F32)
        NST = 4
        st6 = [sb("st6_%d" % i, [128, 12], F32) for i in range(NST)]
        stmv = [sb("stmv_%d" % i, [128, 2], F32) for i in range(NST)]
        stv = [sb("stv_%d" % i, [128, 4], F32) for i in range(NST)]
        stc = [0]
        rf = Ring("rf", [sb("rf%d" % i, [128, 544], F32) for i in range(8)])
        rb = Ring("rb", [sb("rb%d" % i, [128, T], BF16) for i in range(6)])
        ps = Ring("ps", [es.enter_context(nc.psum_tensor("ps%d" % i, [128, 512], F32)) for i in range(8)])

        def pc(pt, r):
            return pcm[:, pt * 40 + r: pt * 40 + r + 1]

        def dma(e, reads, writes, out, in_, key, **kw):
            return op(e, reads, writes, lambda en: en.dma_start(out=out, in_=in_, **kw), dma=key)

        dma("sp", [], ["ident_f"], ident_f[:], dr["c_ident"], "consts")
        dma("sp", [], ["mask"], mask_f[:], dr["c_mask"], "consts")
        dma("sp", [], ["invw"], invw[:], dr["c_invw"], "consts")
        dma("sp", [], ["invcnt"], invcnt[:], dr["c_invcnt"], "consts")
        op("dve", ["ident_f"], ["identb"], lambda en: en.tensor_copy(out=ident_b[:], in_=ident_f[:]))
        op("dve", [], ["ones"], lambda en: en.memset(ones_b[:], 1.0))
        op("dve", [], ["ones_row"], lambda en: en.memset(ones_row[:], 1.0))
        op("pool", [], ["neghalf"], lambda en: en.memset(neghalf[:], -0.5))
        for mc in range(2):
            dma("pool", [], [("x_bf", mc)], x_bf[:, mc, :], dr["mem"][mc * 128:(mc + 1) * 128, :], ("xbf", mc))
        for kc in range(8):
            tb = ps.alloc()
            tbb = tb.t[:].bitcast(BF16)

            def f(en, kc=kc, tbb=tbb):
                for mc in range(2):
                    i = en.transpose(tbb[:, mc * 128:(mc + 1) * 128], x_bf[:, mc, kc * 128:(kc + 1) * 128], ident_b[:])
                return i
            op("pe", [("x_bf", 0), ("x_bf", 1), "identb"], [tb], f)
            op("dve", [tb], ["memT"], lambda en, kc=kc, tbb=tbb: en.tensor_copy(out=memT[:, kc, :], in_=tbb[:, 0:MEM]))

        def stat_tail(si, mvkey):
            mv, sv = stmv[si], stv[si]
            op("dve", [mvkey], [("stv0", si)], lambda en: en.tensor_scalar(
                out=sv[:, 0:1], in0=mv[:, 1:2], scalar1=EPS, scalar2=None, op0=ALU.add))
            op("pool", [("stv0", si), "neghalf"], [("stv1", si)], lambda en: en.tensor_tensor(
                out=sv[:, 1:2], in0=sv[:, 0:1], in1=neghalf[:, 0:1], op=ALU.pow))
            op("dve", [mvkey, ("stv1", si)], [("stv2", si)], lambda en: en.scalar_tensor_tensor(
                out=sv[:, 2:3], in0=mv[:, 0:1], scalar=-1.0, in1=sv[:, 1:2], op0=ALU.mult, op1=ALU.mult))

        def ln_stats(src_ap, nchunk, csz, src_keys):
            si = stc[0] % NST
            stc[0] += 1

            def f(en):
                for c in range(nchunk):
                    i = en.bn_stats(out=st6[si][:, c * 6:(c + 1) * 6], in_=src_ap[:, c * csz:(c + 1) * csz])
                return i
            op("dve", src_keys, [("st6", si)], f)
            op("dve", [("st6", si)], [("stmv", si)], lambda en: en.bn_aggr(out=stmv[si][:], in_=st6[si][:, 0:6 * nchunk]))
            stat_tail(si, ("stmv", si))
            return si

        xT_keys = [("xT", kc) for kc in range(8)]

        def proj(col):
            bank = ps.alloc()

            def f(en):
                for kc in range(8):
                    i = en.matmul(bank.t[:], lhsT=w_in_sb[:, kc, col * 128:(col + 1) * 128], rhs=xT[:, kc, :],
                                  start=(kc == 0), stop=(kc == 7))
                return i
            op("pe", xT_keys + ["w_in"], [bank], f)
            return bank

        def silu_gate(col):
            g = proj(col)
            sg = rb.alloc()
            op("act", [g], [sg], lambda en: en.activation(out=sg.t[:], in_=g.t[:], func=AF.Silu))
            return sg

        try:
          chk("setup0")
          for l in range(nl):
            xsrc = dr["x"] if l == 0 else xscr[l - 1]
            xdst = y_out if l == nl - 1 else xscr[l]

            for kc in range(8):
                for hf in range(2):
                    dma("pool", [], ["w_in"], w_in_sb[:, kc, hf * 1792:(hf + 1) * 1792],
                        dr["w_in"][l][kc * 128:(kc + 1) * 128, hf * 1792:(hf + 1) * 1792], "w_in")
            for kc in range(8):
                dma("pool", [], [("hcat", kc)], hcat[:, kc, :], dr["w_kv"][l][kc * 128:(kc + 1) * 128, :], "w_kv")
            for kc in range(2):
                dma("pool", [], ["w_pw"], w_pw_sb[:, kc, :], dr["cc_pw_w"][l][kc * 128:(kc + 1) * 128, :], "w_pw")
            for g in range(4):
                p0 = (g % 2) * 64
                dma("pool", [], ["poolw"], pool_w_sb[p0:p0 + 64, g // 2, :], dr["pool_w"][l][g], "poolw")
            dma("pool", [], ["bdw"], bdw_row[0:1, :], dr["cc_dw_b"][l:l + 1, :], "bdw")
            for kc in range(10):
                dma("pool", [], ["w_out"], w_out_sb[:, kc, :], dr["w_out"][l][kc * 128:(kc + 1) * 128, :], "w_out")
            dma("sp", [], ["PR"], PR[0:3, :], dr["conv_a_w"][l], "PR")
            dma("sp", [], ["PR"], PR[3:4, :], dr["sg_ln_g"][l:l + 1, :], "PR")
            dma("sp", [], ["PR"], PR[4:5, :], dr["sg_ln_b"][l:l + 1, :], "PR")
            dma("sp", [], ["PR"], PR[5:6, :], dr["pool_scale"][l:l + 1, :], "PR")
            dma("sp", [], ["PR"], PR[6:37, :], dr["cc_dw_w"][l], "PR")
            dma("sp", [], ["PR"], PR[37:38, :], dr["cc_dw_b"][l:l + 1, :], "PR")
            dma("sp", [], ["PR"], PR[38:39, :], dr["cc_ln_g"][l:l + 1, :], "PR")
            dma("sp", [], ["PR"], PR[39:40, :], dr["cc_ln_b"][l:l + 1, :], "PR")
            dma("sp", [], ["lng"], lng_bc[:], dr["ln_g"][l].partition_broadcast(128), "lng")
            dma("sp", [], ["lnb"], lnb_bc[:], dr["ln_b"][l].partition_broadcast(128), "lnb")
            for h in range(4):
                p0 = (h % 2) * 64
                dma("sp", [], ["BSt"], BSt[p0:p0 + 64, h // 2, :], dr["sg_b"][l][h].partition_broadcast(64), "BSt")
            stg = rf.alloc()
            for h in range(4):
                dma("sp", [], [stg], stg.t[:, h * 128:(h + 1) * 128], dr["sg_w"][l][h], "sgw")

            tp = ps.alloc()

            def f(en, tp=tp):
                for pt in range(2):
                    i = en.transpose(tp.t[:, pt * 40:(pt + 1) * 40], PR[0:40, pt * 128:(pt + 1) * 128], ident_f[0:40, 0:40])
                return i
            op("pe", ["PR", "ident_f"], [tp], f)
            op("dve", [tp], ["pcm"], lambda en, tp=tp: en.tensor_copy(out=pcm[:, 0:80], in_=tp.t[:, 0:80]))

            def f(en):
                for pt in range(2):
                    for k in range(31):
                        j = pt * 31 + k
                        i = en.tensor_scalar(out=diag_sb[:, j * 128:(j + 1) * 128], in0=ident_b[:],
                                             scalar1=pc(pt, 6 + k), scalar2=None, op0=ALU.mult)
                return i
            op("dve", ["pcm", "identb"], ["diag"], f)

            def f(en, stg=stg):
                for h in range(4):
                    i = en.tensor_tensor(out=stg.t[:, h * 128:(h + 1) * 128], in0=stg.t[:, h * 128:(h + 1) * 128],
                                         in1=mask_f[:], op=ALU.mult)
                return i
            op("dve", [stg, "mask"], [stg], f)
            tp2 = ps.alloc()

            def f(en, stg=stg, tp2=tp2):
                for h in range(4):
                    i = en.transpose(tp2.t[:, h * 128:(h + 1) * 128], stg.t[:, h * 128:(h + 1) * 128], ident_f[:])
                return i
            op("pe", [stg, "ident_f"], [tp2], f)
            op("dve", [tp2], ["wT"], lambda en, tp2=tp2: en.tensor_copy(out=wT_sb[:], in_=tp2.t[:]))
            rs = [ps.alloc(), ps.alloc()]

            def f(en, rs=rs):
                for h in range(4):
                    p0 = (h % 2) * 64
                    i = en.matmul(rs[h // 2].t[p0:p0 + 64, 0:128], lhsT=ones_b[:, 0:64], rhs=wT_sb[:, h * 128:(h + 1) * 128],
                                  start=True, stop=True)
                return i
            op("pe", ["wT", "ones"], rs, f)
            for pt in range(2):
                def f(en, pt=pt, rs=rs):
                    for rep in range(4):
                        i = en.scalar_tensor_tensor(out=BiasB[:, pt, rep * 128:(rep + 1) * 128], in0=rs[pt].t[:, 0:128],
                                                    scalar=pc(pt, 4), in1=BSt[:, pt, :], op0=ALU.mult, op1=ALU.add)
                    return i
                op("dve", [rs[pt], "pcm", "BSt"], ["biasB"], f)
            hk = [("hcat", kc) for kc in range(8)]
            for pt in range(2):
                kp = ps.alloc()

                def f(en, pt=pt, kp=kp):
                    for kc in range(8):
                        i = en.matmul(kp.t[:, 0:MEM], lhsT=hcat[:, kc, pt * 128:(pt + 1) * 128], rhs=memT[:, kc, :],
                                      start=(kc == 0), stop=(kc == 7))
                    return i
                op("pe", hk + ["memT"], [kp], f)
                op("dve", [kp], ["kT"], lambda en, pt=pt, kp=kp: en.tensor_copy(out=kT_sb[:, pt, :], in_=kp.t[:, 0:MEM]))
            for mc in range(2):
                vp = ps.alloc()

                def f(en, mc=mc, vp=vp):
                    for kc in range(8):
                        i = en.matmul(vp.t[:, 0:DG], lhsT=memT[:, kc, mc * 128:(mc + 1) * 128], rhs=hcat[:, kc, DG:2 * DG],
                                      start=(kc == 0), stop=(kc == 7))
                    return i
                op("pe", hk + ["memT"], [vp], f)
                op("dve", [vp], ["v"], lambda en, mc=mc, vp=vp: en.tensor_copy(out=v_sb[:, mc, :], in_=vp.t[:, 0:DG]))

            def load_x(b):
                for j in range(NT):
                    r0 = b * T + j * 128
                    dma("pool", [("xd", l, b, j)], [("x_bf", j)], x_bf[:, j, :], xsrc[r0:r0 + 128, :], ("xbf", j))

            chk("setupL")
            load_x(0)
            rcount = [0]
            for b in range(NBLK):
                slot, prev = b % 2, 1 - (b % 2)
                for (ext, nm, hl) in ((h_ext, "hext", 30), (xc_ext, "xc", 15), (m_ext, "mext", 2)):
                    wk = [(nm, slot, "halo")]
                    if b == 0:
                        op("pool", [], wk, lambda en, ext=ext, hl=hl: en.memset(ext[slot][:, :, 0:hl], 0.0))
                    else:
                        op("pool", [(nm, prev, 0), (nm, prev, 1)], wk,
                           lambda en, ext=ext, hl=hl: en.tensor_copy(out=ext[slot][:, :, 0:hl], in_=ext[prev][:, :, T:T + hl]))
                for kc in range(8):
                    tb = ps.alloc()
                    tbb = tb.t[:].bitcast(BF16)

                    def f(en, kc=kc, tbb=tbb):
                        for j in range(NT):
                            i = en.transpose(tbb[:, j * 128:(j + 1) * 128], x_bf[:, j, kc * 128:(kc + 1) * 128], ident_b[:])
                        return i
                    op("pe", [("x_bf", j) for j in range(NT)] + ["identb"], [tb], f)
                    if kc % 2 == 0:
                        op("act", [tb], [("xT", kc)], lambda en, kc=kc, tbb=tbb: en.copy(out=xT[:, kc, :], in_=tbb[:, 0:T]))
                    else:
                        op("dve", [tb], [("xT", kc)], lambda en, kc=kc, tbb=tbb: en.tensor_copy(out=xT[:, kc, :], in_=tbb[:, 0:T]))
                chk("xT")
                if b + 1 < NBLK:
                    load_x(b + 1)

                da = [proj(12), proj(13)]
                dg = [proj(14), proj(15)]
                for pt in range(2):
                    sig = rf.alloc()
                    op("act", [dg[pt]], [sig], lambda en, pt=pt, sig=sig: en.activation(
                        out=sig.t[:, 0:T], in_=dg[pt].t[:], func=AF.Sigmoid))
                    op("dve", [da[pt], sig], [("hext", slot, pt)], lambda en, pt=pt, sig=sig: en.tensor_tensor(
                        out=h_ext[slot][:, pt, 30:30 + T], in0=da[pt].t[:], in1=sig.t[:, 0:T], op=ALU.mult))
                chk("D1")
                for j in range(NT):
                    vps = ps.alloc()

                    def f(en, j=j, vps=vps):
                        for kc in range(8):
                            i = en.matmul(vps.t[:, 0:DG], lhsT=xT[:, kc, j * 128:(j + 1) * 128], rhs=w_in_sb[:, kc, 1024:1280],
                                          start=(kc == 0), stop=(kc == 7))
                        return i
                    op("pe", xT_keys + ["w_in"], [vps], f)
                    gv = rf.alloc()
                    op("act", [vps], [gv], lambda en, vps=vps, gv=gv: en.activation(
                        out=gv.t[:, 0:DG], in_=vps.t[:, 0:DG], func=AF.Gelu_apprx_tanh))
                    si = ln_stats(gv.t, 1, DG, [gv])
                    op("act", [gv, ("stv1", si), ("stv2", si)], [("vn", j)], lambda en, j=j, gv=gv, si=si: en.activation(
                        out=vn_tm[j][:], in_=gv.t[:, 0:DG], func=AF.Identity, scale=stv[si][:, 1:2], bias=stv[si][:, 2:3]))
                chk("Bv")
                hkeys = [("hext", slot, "halo"), ("hext", slot, 0), ("hext", slot, 1)]
                for j in range(NT):
                    cps = ps.alloc()

                    def f(en, j=j, cps=cps):
                        for pt in range(2):
                            for k in range(31):
                                en.matmul(cps.t[:, pt * 128:(pt + 1) * 128], lhsT=h_ext[slot][:, pt, j * 128 + k: j * 128 + k + 128],
                                          rhs=diag_sb[:, (pt * 31 + k) * 128:(pt * 31 + k + 1) * 128], start=(k == 0), stop=False)
                            i = en.matmul(cps.t[:, pt * 128:(pt + 1) * 128], lhsT=ones_row[0:1, 0:128],
                                          rhs=bdw_row[0:1, pt * 128:(pt + 1) * 128], start=False, stop=True)
                        return i
                    op("pe", hkeys + ["diag", "bdw", "ones_row"], [cps], f)
                    si = ln_stats(cps.t, 1, DG, [cps])
                    op("act", [cps, ("stv1", si), ("stv2", si)], [("hn", j)], lambda en, j=j, cps=cps, si=si: en.activation(
                        out=hn_tm[j][:], in_=cps.t[:, 0:DG], func=AF.Identity, scale=stv[si][:, 1:2], bias=stv[si][:, 2:3]))
                chk("Dconv")
                for pt in range(2):
                    ups = proj(6 + pt)
                    op("act", [ups], [("gu", pt)], lambda en, pt=pt, ups=ups: en.activation(
                        out=gelu_u[:, pt, :], in_=ups.t[:], func=AF.Gelu_apprx_tanh))
                mp = [ps.alloc(), ps.alloc()]

                def f(en, mp=mp):
                    for j in range(NT):
                        for h in range(4):
                            p0 = (h % 2) * 64
                            i = en.matmul(mp[h // 2].t[p0:p0 + 64, j * 128:(j + 1) * 128], lhsT=vn_tm[j][:, h * 64:(h + 1) * 64],
                                          rhs=wT_sb[:, h * 128:(h + 1) * 128], start=True, stop=True)
                    return i
                op("pe", [("vn", j) for j in range(NT)] + ["wT"], mp, f)
                chk("Bmix")
                tb = ps.alloc()
                tbb = tb.t[:].bitcast(BF16)

                def f(en, tbb=tbb):
                    for j in range(NT):
                        for pt in range(2):
                            i = en.transpose(tbb[:, pt * T + j * 128: pt * T + (j + 1) * 128], hn_tm[j][:, pt * 128:(pt + 1) * 128], ident_b[:])
                    return i
                op("pe", [("hn", j) for j in range(NT)] + ["identb"], [tb], f)
                sfm = []
                for pt in range(2):
                    s_ = rb.alloc()
                    op("act", [tb, "pcm"], [s_], lambda en, pt=pt, s_=s_, tbb=tbb: en.activation(
                        out=s_.t[:], in_=tbb[:, pt * T:(pt + 1) * T], func=AF.Silu, scale=pc(pt, 38), bias=pc(pt, 39)))
                    sfm.append(s_)
                for po in range(2):
                    pw = ps.alloc()

                    def f(en, po=po, pw=pw, sfm=sfm):
                        for kc in range(2):
                            i = en.matmul(pw.t[:], lhsT=w_pw_sb[:, kc, po * 128:(po + 1) * 128], rhs=sfm[kc].t[:],
                                          start=(kc == 0), stop=(kc == 1))
                        return i
                    op("pe", sfm + ["w_pw"], [pw], f)
                    sg = silu_gate(24 + po)
                    op("dve", [pw, sg], [("hcat", 6 + po)], lambda en, po=po, pw=pw, sg=sg: en.tensor_tensor(
                        out=hcat[:, 6 + po, :], in0=pw.t[:], in1=sg.t[:], op=ALU.mult))
                chk("Dtail")
                for pt in range(2):
                    sg = silu_gate(20 + pt)
                    op("dve", [sg, ("gu", pt)], [sg], lambda en, pt=pt, sg=sg: en.tensor_tensor(
                        out=sg.t[:], in0=sg.t[:], in1=gelu_u[:, pt, :], op=ALU.mult))
                    z = rf.alloc()
                    op("dve", [mp[pt], "pcm", "biasB"], [z], lambda en, pt=pt, z=z, mp=mp: en.scalar_tensor_tensor(
                        out=z.t[:, 0:T], in0=mp[pt].t[:], scalar=pc(pt, 3), in1=BiasB[:, pt, :], op0=ALU.mult, op1=ALU.add))
                    op("dve", [z, sg], [("hcat", 2 + pt)], lambda en, pt=pt, z=z, sg=sg: en.tensor_tensor(
                        out=hcat[:, 2 + pt, :], in0=z.t[:, 0:T], in1=sg.t[:], op=ALU.mult))
                chk("Bend")
                for pt in range(2):
                    qps = proj(16 + pt)
                    op("act", [qps], [("q", pt)], lambda en, pt=pt, qps=qps: en.copy(out=q_sb[:, pt, :], in_=qps.t[:]))
                for pair in range(2):
                    o_ps = ps.alloc()
                    d_ps = ps.alloc()
                    for hh in range(2):
                        h = pair * 2 + hh
                        p0 = hh * 64
                        pTs = []
                        for mc in range(2):
                            sps = ps.alloc()
                            op("pe", [("q", pair), "kT"], [sps], lambda en, mc=mc, sps=sps, p0=p0: en.matmul(
                                sps.t[:], lhsT=kT_sb[p0:p0 + 64, pair, mc * 128:(mc + 1) * 128], rhs=q_sb[p0:p0 + 64, pair, :],
                                start=True, stop=True))
                            pT = rb.alloc()
                            op("act", [sps], [pT], lambda en, sps=sps, pT=pT: en.activation(
                                out=pT.t[:], in_=sps.t[:], func=AF.Exp, scale=0.125))
                            pTs.append(pT)

                        def f(en, h=h, p0=p0, pTs=pTs, o_ps=o_ps, d_ps=d_ps):
                            for mc in range(2):
                                en.matmul(o_ps.t[p0:p0 + 64, :], lhsT=v_sb[:, mc, h * 64:(h + 1) * 64], rhs=pTs[mc].t[:],
                                          start=(mc == 0), stop=(mc == 1))
                            for mc in range(2):
                                i = en.matmul(d_ps.t[p0:p0 + 64, :], lhsT=ones_b[:, 0:64], rhs=pTs[mc].t[:],
                                              start=(mc == 0), stop=(mc == 1))
                            return i
                        op("pe", pTs + ["v", "ones"], [o_ps, d_ps], f)
                    rden = rf.alloc()
                    op("dve", [d_ps], [rden], lambda en, rden=rden, d_ps=d_ps: en.reciprocal(out=rden.t[:, 0:T], in_=d_ps.t[:]))
                    t_ = rf.alloc()
                    op("dve", [o_ps, rden], [t_], lambda en, t_=t_, o_ps=o_ps, rden=rden: en.tensor_tensor(
                        out=t_.t[:, 0:T], in0=o_ps.t[:], in1=rden.t[:, 0:T], op=ALU.mult))
                    sg = silu_gate(26 + pair)
                    op("dve", [t_, sg], [("hcat", 8 + pair)], lambda en, pair=pair, t_=t_, sg=sg: en.tensor_tensor(
                        out=hcat[:, 8 + pair, :], in0=t_.t[:, 0:T], in1=sg.t[:], op=ALU.mult))
                chk("E")
                X = xc_ext[slot]
                W = T + 15
                for pt in range(2):
                    xps = proj(10 + pt)
                    op("act", [xps], [("xc", slot, pt)], lambda en, pt=pt, xps=xps: en.copy(out=X[:, pt, 15:W], in_=xps.t[:]))
                xck = [("xc", slot, "halo"), ("xc", slot, 0), ("xc", slot, 1)]
                S2 = [rf.alloc(), rf.alloc()]
                S4 = [rf.alloc(), rf.alloc()]
                for pt in range(2):
                    op("dve", xck, [S2[pt]], lambda en, pt=pt: en.tensor_tensor(
                        out=S2[pt].t[:, 1:W], in0=X[:, pt, 1:W], in1=X[:, pt, 0:W - 1], op=ALU.add))
                op("dve", [S2[0]], [S4[0]], lambda en: en.tensor_tensor(
                    out=S4[0].t[64:128, 3:W], in0=S2[0].t[64:128, 3:W], in1=S2[0].t[64:128, 1:W - 2], op=ALU.add))
                op("dve", [S2[1]], [S4[1]], lambda en: en.tensor_tensor(
                    out=S4[1].t[:, 3:W], in0=S2[1].t[:, 3:W], in1=S2[1].t[:, 1:W - 2], op=ALU.add))
                op("dve", [S4[1]], [S2[1]], lambda en: en.tensor_tensor(
                    out=S2[1].t[:, 7:W], in0=S4[1].t[:, 7:W], in1=S4[1].t[:, 3:W - 4], op=ALU.add))
                op("dve", [S2[1]], [S4[1]], lambda en: en.tensor_tensor(
                    out=S4[1].t[64:128, 15:W], in0=S2[1].t[64:128, 15:W], in1=S2[1].t[64:128, 7:W - 8], op=ALU.add))
                yp = [rb.alloc(), rb.alloc()]
                srcs = [(0, 0, S2[0]), (0, 64, S4[0]), (1, 0, S2[1]), (1, 64, S4[1])]
                for (pt, p0, Sx) in srcs:
                    op("dve", [Sx, "invw"] + xck, [yp[pt]], lambda en, pt=pt, p0=p0, Sx=Sx: en.scalar_tensor_tensor(
                        out=yp[pt].t[p0:p0 + 64, :], in0=Sx.t[p0:p0 + 64, 15:W], scalar=invw[p0:p0 + 64, pt:pt + 1],
                        in1=X[p0:p0 + 64, pt, 15:W], op0=ALU.mult, op1=ALU.subtract))
                    if b == 0:
                        op("dve", [Sx, "invcnt"], ["ctmp"], lambda en, pt=pt, p0=p0, Sx=Sx: en.tensor_tensor(
                            out=ctmp[p0:p0 + 64, :], in0=Sx.t[p0:p0 + 64, 15:31], in1=invcnt[p0:p0 + 64, pt * 16:(pt + 1) * 16], op=ALU.mult))
                        op("dve", ["ctmp"] + xck, [yp[pt]], lambda en, pt=pt, p0=p0: en.tensor_tensor(
                            out=yp[pt].t[p0:p0 + 64, 0:16], in0=ctmp[p0:p0 + 64, :], in1=X[p0:p0 + 64, pt, 15:31], op=ALU.subtract))
                cp = [ps.alloc(), ps.alloc()]
                for g in range(4):
                    pt, p0 = g // 2, (g % 2) * 64
                    op("pe", [yp[pt], "poolw"], [cp[pt]], lambda en, pt=pt, p0=p0: en.matmul(
                        cp[pt].t[p0:p0 + 64, :], lhsT=pool_w_sb[p0:p0 + 64, pt, :], rhs=yp[pt].t[p0:p0 + 64, :], start=True, stop=True))
                for pt in range(2):
                    sg = silu_gate(22 + pt)
                    op("dve", [cp[pt], sg, "pcm"], [("hcat", 4 + pt)], lambda en, pt=pt, sg=sg, cp=cp: en.scalar_tensor_tensor(
                        out=hcat[:, 4 + pt, :], in0=cp[pt].t[:], scalar=pc(pt, 5), in1=sg.t[:], op0=ALU.mult, op1=ALU.mult))
                chk("C")
                M = m_ext[slot]
                xa = [proj(0), proj(1)]
                ca = [proj(4), proj(5)]
                for pt in range(2):
                    xs = rf.alloc()
                    op("act", [xa[pt]], [xs], lambda en, pt=pt, xs=xs: en.copy(out=xs.t[:, 0:T], in_=xa[pt].t[:]))
                    op("dve", [ca[pt], xs], [("mext", slot, pt)], lambda en, pt=pt, xs=xs: en.tensor_tensor(
                        out=M[:, pt, 2:2 + T], in0=ca[pt].t[:], in1=xs.t[:, 0:T], op=ALU.mult))
                mk = [("mext", slot, "halo"), ("mext", slot, 0), ("mext", slot, 1)]
                for pt in range(2):
                    acc = rf.alloc()
                    op("dve", mk + ["pcm"], [acc], lambda en, pt=pt, acc=acc: en.tensor_scalar(
                        out=acc.t[:, 0:T], in0=M[:, pt, 2:2 + T], scalar1=pc(pt, 2), scalar2=None, op0=ALU.mult))
                    op("dve", mk + ["pcm", acc], [acc], lambda en, pt=pt, acc=acc: en.scalar_tensor_tensor(
                        out=acc.t[:, 0:T], in0=M[:, pt, 1:1 + T], scalar=pc(pt, 1), in1=acc.t[:, 0:T], op0=ALU.mult, op1=ALU.add))
                    op("dve", mk + ["pcm", acc], [acc], lambda en, pt=pt, acc=acc: en.scalar_tensor_tensor(
                        out=acc.t[:, 0:T], in0=M[:, pt, 0:T], scalar=pc(pt, 0), in1=acc.t[:, 0:T], op0=ALU.mult, op1=ALU.add))
                    bps = proj(2 + pt)
                    op("dve", [bps, acc], [acc], lambda en, acc=acc, bps=bps: en.tensor_tensor(
                        out=acc.t[:, 0:T], in0=bps.t[:], in1=acc.t[:, 0:T], op=ALU.mult))
                    sg = silu_gate(18 + pt)
                    op("dve", [acc, sg], [("hcat", pt)], lambda en, pt=pt, acc=acc, sg=sg: en.tensor_tensor(
                        out=hcat[:, pt, :], in0=acc.t[:, 0:T], in1=sg.t[:], op=ALU.mult))
                chk("A")
                hck = [("hcat", kc) for kc in range(10)]
                for j in range(NT):
                    rs_ = rcount[0] % 2
                    rcount[0] += 1
                    r = r_buf[rs_]
                    rk = ("r", rs_)
                    r0 = b * T + j * 128
                    dma("sp", [("xd", l, b, j)], [rk], r[:], xsrc[r0:r0 + 128, :], ("rld", rs_))
                    yb = [ps.alloc(), ps.alloc()]

                    def f(en, j=j, yb=yb):
                        for hf in range(2):
                            for kc in range(10):
                                i = en.matmul(yb[hf].t[:], lhsT=hcat[:, kc, j * 128:(j + 1) * 128], rhs=w_out_sb[:, kc, hf * 512:(hf + 1) * 512],
                                              start=(kc == 0), stop=(kc == 9))
                        return i
                    op("pe", hck + ["w_out"], yb, f)

                    def f(en, r=r, yb=yb):
                        for hf in range(2):
                            i = en.scalar_tensor_tensor(out=r[:, hf * 512:(hf + 1) * 512], in0=r[:, hf * 512:(hf + 1) * 512], scalar=ALPHA,
                                                        in1=yb[hf].t[:], op0=ALU.mult, op1=ALU.add)
                        return i
                    op("dve", [rk] + yb, [rk], f)
                    si = ln_stats(r, 2, 512, [rk])
                    op("act", [rk, ("stv1", si), ("stv2", si)], [rk], lambda en, r=r, si=si: en.activation(
                        out=r[:], in_=r[:], func=AF.Identity, scale=stv[si][:, 1:2], bias=stv[si][:, 2:3]))
                    op("pool", [rk, "lng"], [rk], lambda en, r=r: en.tensor_tensor(out=r[:], in0=r[:], in1=lng_bc[:], op=ALU.mult))
                    op("pool", [rk, "lnb"], [rk], lambda en, r=r: en.tensor_tensor(out=r[:], in0=r[:], in1=lnb_bc[:], op=ALU.add))
                    dma("sp", [rk], [("xd", l + 1, b, j)], xdst[r0:r0 + 128, :], r[:], ("rst", rs_))
                chk("blk%d" % b)
        except StopBuild:
            pass
        for e in tr.engs:
            if tr.cnt[e] > 0:
                tr._wait("sp", (tr.sem[e], tr.cnt[e], e))
        for k, s in tr.dsem.items():
            tr._wait("sp", (s[0], s[1], "dma:" + str(k)))
    return nc


def _consts():
    ident = np.eye(128, dtype=np.float32)
    mask = np.tril(np.ones((128, 128), dtype=np.float32))
    wins = [2, 4, 8, 16]
    invw = np.zeros((128, 2), np.float32)
    invcnt = np.zeros((128, 32), np.float32)
    for pt in range(2):
        for half in range(2):
            w = wins[pt * 2 + half]
            invw[half * 64:(half + 1) * 64, pt] = 1.0 / w
            for t in range(16):
                invcnt[half * 64:(half + 1) * 64, pt * 16 + t] = 1.0 / min(t + 1, w)
    return {"c_ident": ident, "c_mask": mask, "c_invw": invw, "c_invcnt": invcnt}


_NC_CACHE = {}
FUSED = False


def kernel(**inputs):
    x = np.ascontiguousarray(inputs["x"], dtype=np.float32)
    mem = np.ascontiguousarray(inputs["mem"], dtype=np.float32)
    consts = _consts()
    L = inputs["w_in"].shape[0]
    if FUSED:
        if L not in _NC_CACHE:
            _NC_CACHE[L] = build(L)
        nc = _NC_CACHE[L]
        in_maps = []
        for c in range(N_CORES):
            m = {"x": x[c], "mem": mem[c]}
            for p in PARAMS:
                m[p] = np.ascontiguousarray(inputs[p], dtype=np.float32)
            m.update(consts)
            in_maps.append(m)
        res = run_bass_kernel_spmd(nc, in_maps, core_ids=list(range(N_CORES)))
        return np.stack([np.asarray(r["y"]) for r in res.results], axis=0).astype(np.float32)
    if 1 not in _NC_CACHE:
        _NC_CACHE[1] = build(1)
    nc = _NC_CACHE[1]
    cur = x
    for l in range(L):
        in_maps = []
        for c in range(N_CORES):
            m = {"x": np.ascontiguousarray(cur[c]), "mem": mem[c]}
            for p in PARAMS:
                m[p] = np.ascontiguousarray(inputs[p][l:l + 1], dtype=np.float32)
            m.update(consts)
            in_maps.append(m)
        res = run_bass_kernel_spmd(nc, in_maps, core_ids=list(range(N_CORES)))
        cur = np.stack([np.asarray(r["y"]) for r in res.results], axis=0).astype(np.float32)
    return cur
```

```python
import numpy as np
from contextlib import ExitStack
import concourse.bass as bass
import concourse.mybir as mybir
from concourse.bass_utils import run_bass_kernel_spmd

F32, BF16 = mybir.dt.float32, mybir.dt.bfloat16
AF = mybir.ActivationFunctionType
ALU = mybir.AluOpType

D = 1024; S = 4096; DG = 256; DIN = 3584; DMIX = 1280; MEM = 256
T = 512; NT = 4; NBLK = S // T
ALPHA = float((2.0 * 2) ** 0.25)
EPS = 1e-5
N_CORES = 8
PARAMS = ["w_in", "conv_a_w", "sg_ln_g", "sg_ln_b", "sg_w", "sg_b", "pool_w", "pool_scale",
          "cc_dw_w", "cc_dw_b", "cc_ln_g", "cc_ln_b", "cc_pw_w", "w_kv", "w_out", "ln_g", "ln_b"]
PSHAPES = {"w_in": [D, DIN], "conv_a_w": [3, DG], "sg_ln_g": [DG], "sg_ln_b": [DG], "sg_w": [4, 128, 128],
           "sg_b": [4, 128], "pool_w": [4, 64, 64], "pool_scale": [DG], "cc_dw_w": [31, DG], "cc_dw_b": [DG],
           "cc_ln_g": [DG], "cc_ln_b": [DG], "cc_pw_w": [DG, DG], "w_kv": [D, 2 * DG], "w_out": [DMIX, D],
           "ln_g": [D], "ln_b": [D]}


class H:
    __slots__ = ("t", "key", "ring", "slot", "gen")

    def __init__(self, t, key, ring=None, slot=0, gen=0):
        self.t, self.key, self.ring, self.slot, self.gen = t, key, ring, slot, gen


class Ring:
    def __init__(self, name, tensors):
        self.name, self.tensors, self.i = name, tensors, 0
        self.gen = [0] * len(tensors)

    def alloc(self):
        s = self.i % len(self.tensors)
        self.i += 1
        self.gen[s] += 1
        return H(self.tensors[s], (self.name, s), self, s, self.gen[s])


class Tr:
    def __init__(self, nc, es):
        self.nc, self.es = nc, es
        self.engs = {"pe": nc.tensor, "act": nc.scalar, "dve": nc.vector, "pool": nc.gpsimd, "sp": nc.sync}
        self.sem = {e: es.enter_context(nc.semaphore("s_" + e)) for e in self.engs}
        self.cnt = {e: 0 for e in self.engs}
        self.waited = {e: {} for e in self.engs}
        self.lastw = {}
        self.readers = {}
        self.dsem = {}

    def _key(self, b):
        if isinstance(b, H):
            assert b.ring.gen[b.slot] == b.gen, f"ring buffer {b.key} reused while live"
            return b.key
        return b

    def _wait(self, e, tok):
        sem, val, sid = tok
        if self.waited[e].get(sid, 0) >= val:
            return
        self.engs[e].wait_ge(sem, val)
        self.waited[e][sid] = val

    def op(self, e, reads, writes, fn, dma=None):
        rk = [self._key(b) for b in reads]
        wk = [self._key(b) for b in writes]
        deps = []
        for k in rk:
            if k in self.lastw:
                deps.append(self.lastw[k])
        for k in wk:
            if k in self.lastw:
                deps.append(self.lastw[k])
            deps.extend(self.readers.get(k, {}).values())
        for tok in deps:
            if e == "pe" and tok[2] == "pe":
                continue
            self._wait(e, tok)
        inst = fn(self.engs[e])
        if dma is None:
            self.cnt[e] += 1
            inst.then_inc(self.sem[e], 1)
            tok = (self.sem[e], self.cnt[e], e)
        else:
            if dma not in self.dsem:
                self.dsem[dma] = [self.es.enter_context(self.nc.semaphore("d_%d" % len(self.dsem))), 0]
            s = self.dsem[dma]
            s[1] += 16
            inst.then_inc(s[0], 16)
            tok = (s[0], s[1], "dma:" + str(dma))
        for k in rk:
            d = self.readers.setdefault(k, {})
            old = d.get(tok[2])
            if old is None or old[1] < tok[1]:
                d[tok[2]] = tok
        for k in wk:
            self.lastw[k] = tok
            self.readers[k] = {}
        return tok


class StopBuild(Exception):
    pass


STOP = None


def chk(name):
    if STOP == name:
        raise StopBuild()


def build(nl):
    nc = bass.Bass("TRN2", target_bir_lowering=False)
    dr = {}
    dr["x"] = nc.dram_tensor("x", [S, D], F32, kind="ExternalInput").ap()
    dr["mem"] = nc.dram_tensor("mem", [MEM, D], F32, kind="ExternalInput").ap()
    for p in PARAMS:
        dr[p] = nc.dram_tensor(p, [nl] + PSHAPES[p], F32, kind="ExternalInput").ap()
    dr["c_ident"] = nc.dram_tensor("c_ident", [128, 128], F32, kind="ExternalInput").ap()
    dr["c_mask"] = nc.dram_tensor("c_mask", [128, 128], F32, kind="ExternalInput").ap()
    dr["c_invw"] = nc.dram_tensor("c_invw", [128, 2], F32, kind="ExternalInput").ap()
    dr["c_invcnt"] = nc.dram_tensor("c_invcnt", [128, 32], F32, kind="ExternalInput").ap()
    y_out = nc.dram_tensor("y", [S, D], F32, kind="ExternalOutput").ap()
    xscr = [nc.dram_tensor("xscr%d" % i, [S, D], F32, kind="Internal").ap() for i in range(nl - 1)]

    es = ExitStack()
    with es:
        def sb(name, shape, dt):
            return es.enter_context(nc.sbuf_tensor(name, shape, dt))

        tr = Tr(nc, es)
        op = tr.op
        ident_f = sb("ident_f", [128, 128], F32); ident_b = sb("ident_b", [128, 128], BF16)
        mask_f = sb("mask_f", [128, 128], F32)
        invw = sb("invw", [128, 2], F32); invcnt = sb("invcnt", [128, 32], F32)
        ones_b = sb("ones_b", [128, 128], BF16); neghalf = sb("neghalf", [128, 4], F32)
        ones_row = sb("ones_row", [1, 128], BF16)
        w_in_sb = sb("w_in_sb", [128, 8, DIN], BF16)
        w_out_sb = sb("w_out_sb", [128, 10, D], BF16)
        w_pw_sb = sb("w_pw_sb", [128, 2, DG], BF16)
        pool_w_sb = sb("pool_w_sb", [128, 2, 64], BF16)
        PR = sb("PR", [40, DG], F32)
        pcm = sb("pcm", [128, 80], F32)
        bdw_row = sb("bdw_row", [1, DG], BF16)
        lng_bc = sb("lng_bc", [128, D], F32); lnb_bc = sb("lnb_bc", [128, D], F32)
        wT_sb = sb("wT_sb", [128, 512], BF16)
        BSt = sb("BSt", [128, 2, 128], F32)
        BiasB = sb("BiasB", [128, 2, 512], F32)
        diag_sb = sb("diag_sb", [128, 62 * 128], BF16)
        memT = sb("memT", [128, 8, MEM], BF16)
        kT_sb = sb("kT_sb", [128, 2, MEM], BF16)
        v_sb = sb("v_sb", [128, 2, DG], BF16)
        x_bf = sb("x_bf", [128, NT, D], BF16)
        xT = sb("xT", [128, 8, T], BF16)
        hcat = sb("hcat", [128, 10, T], BF16)
        m_ext = [sb("m_ext%d" % i, [128, 2, T + 2], F32) for i in range(2)]
        xc_ext = [sb("xc_ext%d" % i, [128, 2, T + 15], F32) for i in range(2)]
        h_ext = [sb("h_ext%d" % i, [128, 2, T + 30], BF16) for i in range(2)]
        r_buf = [sb("r_buf%d" % i, [128, D], F32) for i in range(2)]
        vn_tm = [sb("vn_tm%d" % i, [128, DG], BF16) for i in range(NT)]
        hn_tm = [sb("hn_tm%d" % i, [128, DG], BF16) for i in range(NT)]
        gelu_u = sb("gelu_u", [128, 2, T], BF16)
        q_sb = sb("q_sb", [128, 2, T], BF16)
        ctmp = sb("ctmp", [128, 16], F32)
        NST = 4
        st6 = [sb("st6_%d" % i, [128, 12], F32) for i in range(NST)]
        stmv = [sb("stmv_%d" % i, [128, 2], F32) for i in range(NST)]
        stv = [sb("stv_%d" % i, [128, 4], F32) for i in range(NST)]
        stc = [0]
        rf = Ring("rf", [sb("rf%d" % i, [128, 544], F32) for i in range(8)])
        rb = Ring("rb", [sb("rb%d" % i, [128, T], BF16) for i in range(6)])
        ps = Ring("ps", [es.enter_context(nc.psum_tensor("ps%d" % i, [128, 512], F32)) for i in range(8)])

        def pc(pt, r):
            return pcm[:, pt * 40 + r: pt * 40 + r + 1]

        def dma(e, reads, writes, out, in_, key, **kw):
            return op(e, reads, writes, lambda en: en.dma_start(out=out, in_=in_, **kw), dma=key)

        dma("sp", [], ["ident_f"], ident_f[:], dr["c_ident"], "consts")
        dma("sp", [], ["mask"], mask_f[:], dr["c_mask"], "consts")
        dma("sp", [], ["invw"], invw[:], dr["c_invw"], "consts")
        dma("sp", [], ["invcnt"], invcnt[:], dr["c_invcnt"], "consts")
        op("dve", ["ident_f"], ["identb"], lambda en: en.tensor_copy(out=ident_b[:], in_=ident_f[:]))
        op("dve", [], ["ones"], lambda en: en.memset(ones_b[:], 1.0))
        op("dve", [], ["ones_row"], lambda en: en.memset(ones_row[:], 1.0))
        op("pool", [], ["neghalf"], lambda en: en.memset(neghalf[:], -0.5))
        for mc in range(2):
            dma("pool", [], [("x_bf", mc)], x_bf[:, mc, :], dr["mem"][mc * 128:(mc + 1) * 128, :], ("xbf", mc))
        for kc in range(8):
            tb = ps.alloc()
            tbb = tb.t[:].bitcast(BF16)

            def f(en, kc=kc, tbb=tbb):
                for mc in range(2):
                    i = en.transpose(tbb[:, mc * 128:(mc + 1) * 128], x_bf[:, mc, kc * 128:(kc + 1) * 128], ident_b[:])
                return i
            op("pe", [("x_bf", 0), ("x_bf", 1), "identb"], [tb], f)
            op("dve", [tb], ["memT"], lambda en, kc=kc, tbb=tbb: en.tensor_copy(out=memT[:, kc, :], in_=tbb[:, 0:MEM]))

        def stat_tail(si, mvkey):
            mv, sv = stmv[si], stv[si]
            op("dve", [mvkey], [("stv0", si)], lambda en: en.tensor_scalar(
                out=sv[:, 0:1], in0=mv[:, 1:2], scalar1=EPS, scalar2=None, op0=ALU.add))
            op("pool", [("stv0", si), "neghalf"], [("stv1", si)], lambda en: en.tensor_tensor(
                out=sv[:, 1:2], in0=sv[:, 0:1], in1=neghalf[:, 0:1], op=ALU.pow))
            op("dve", [mvkey, ("stv1", si)], [("stv2", si)], lambda en: en.scalar_tensor_tensor(
                out=sv[:, 2:3], in0=mv[:, 0:1], scalar=-1.0, in1=sv[:, 1:2], op0=ALU.mult, op1=ALU.mult))

        def ln_stats(src_ap, nchunk, csz, src_keys):
            si = stc[0] % NST
            stc[0] += 1

            def f(en):
                for c in range(nchunk):
                    i = en.bn_stats(out=st6[si][:, c * 6:(c + 1) * 6], in_=src_ap[:, c * csz:(c + 1) * csz])
                return i
            op("dve", src_keys, [("st6", si)], f)
            op("dve", [("st6", si)], [("stmv", si)], lambda en: en.bn_aggr(out=stmv[si][:], in_=st6[si][:, 0:6 * nchunk]))
            stat_tail(si, ("stmv", si))
            return si

        xT_keys = [("xT", kc) for kc in range(8)]

        def proj(col):
            bank = ps.alloc()

            def f(en):
                for kc in range(8):
                    i = en.matmul(bank.t[:], lhsT=w_in_sb[:, kc, col * 128:(col + 1) * 128], rhs=xT[:, kc, :],
                                  start=(kc == 0), stop=(kc == 7))
                return i
            op("pe", xT_keys + ["w_in"], [bank], f)
            return bank

        def silu_gate(col):
            g = proj(col)
            sg = rb.alloc()
            op("act", [g], [sg], lambda en: en.activation(out=sg.t[:], in_=g.t[:], func=AF.Silu))
            return sg

        try:
          chk("setup0")
          for l in range(nl):
            xsrc = dr["x"] if l == 0 else xscr[l - 1]
            xdst = y_out if l == nl - 1 else xscr[l]

            for kc in range(8):
                for hf in range(2):
                    dma("pool", [], ["w_in"], w_in_sb[:, kc, hf * 1792:(hf + 1) * 1792],
                        dr["w_in"][l][kc * 128:(kc + 1) * 128, hf * 1792:(hf + 1) * 1792], "w_in")
            for kc in range(8):
                dma("pool", [], [("hcat", kc)], hcat[:, kc, :], dr["w_kv"][l][kc * 128:(kc + 1) * 128, :], "w_kv")
            for kc in range(2):
                dma("pool", [], ["w_pw"], w_pw_sb[:, kc, :], dr["cc_pw_w"][l][kc * 128:(kc + 1) * 128, :], "w_pw")
            for g in range(4):
                p0 = (g % 2) * 64
                dma("pool", [], ["poolw"], pool_w_sb[p0:p0 + 64, g // 2, :], dr["pool_w"][l][g], "poolw")
            dma("pool", [], ["bdw"], bdw_row[0:1, :], dr["cc_dw_b"][l:l + 1, :], "bdw")
            for kc in range(10):
                dma("pool", [], ["w_out"], w_out_sb[:, kc, :], dr["w_out"][l][kc * 128:(kc + 1) * 128, :], "w_out")
            dma("sp", [], ["PR"], PR[0:3, :], dr["conv_a_w"][l], "PR")
            dma("sp", [], ["PR"], PR[3:4, :], dr["sg_ln_g"][l:l + 1, :], "PR")
            dma("sp", [], ["PR"], PR[4:5, :], dr["sg_ln_b"][l:l + 1, :], "PR")
            dma("sp", [], ["PR"], PR[5:6, :], dr["pool_scale"][l:l + 1, :], "PR")
            dma("sp", [], ["PR"], PR[6:37, :], dr["cc_dw_w"][l], "PR")
            dma("sp", [], ["PR"], PR[37:38, :], dr["cc_dw_b"][l:l + 1, :], "PR")
            dma("sp", [], ["PR"], PR[38:39, :], dr["cc_ln_g"][l:l + 1, :], "PR")
            dma("sp", [], ["PR"], PR[39:40, :], dr["cc_ln_b"][l:l + 1, :], "PR")
            dma("sp", [], ["lng"], lng_bc[:], dr["ln_g"][l].partition_broadcast(128), "lng")
            dma("sp", [], ["lnb"], lnb_bc[:], dr["ln_b"][l].partition_broadcast(128), "lnb")
            for h in range(4):
                p0 = (h % 2) * 64
                dma("sp", [], ["BSt"], BSt[p0:p0 + 64, h // 2, :], dr["sg_b"][l][h].partition_broadcast(64), "BSt")
            stg = rf.alloc()
            for h in range(4):
                dma("sp", [], [stg], stg.t[:, h * 128:(h + 1) * 128], dr["sg_w"][l][h], "sgw")

            tp = ps.alloc()

            def f(en, tp=tp):
                for pt in range(2):
                    i = en.transpose(tp.t[:, pt * 40:(pt + 1) * 40], PR[0:40, pt * 128:(pt + 1) * 128], ident_f[0:40, 0:40])
                return i
            op("pe", ["PR", "ident_f"], [tp], f)
            op("dve", [tp], ["pcm"], lambda en, tp=tp: en.tensor_copy(out=pcm[:, 0:80], in_=tp.t[:, 0:80]))

            def f(en):
                for pt in range(2):
                    for k in range(31):
                        j = pt * 31 + k
                        i = en.tensor_scalar(out=diag_sb[:, j * 128:(j + 1) * 128], in0=ident_b[:],
                                             scalar1=pc(pt, 6 + k), scalar2=None, op0=ALU.mult)
                return i
            op("dve", ["pcm", "identb"], ["diag"], f)

            def f(en, stg=stg):
                for h in range(4):
                    i = en.tensor_tensor(out=stg.t[:, h * 128:(h + 1) * 128], in0=stg.t[:, h * 128:(h + 1) * 128],
                                         in1=mask_f[:], op=ALU.mult)
                return i
            op("dve", [stg, "mask"], [stg], f)
            tp2 = ps.alloc()

            def f(en, stg=stg, tp2=tp2):
                for h in range(4):
                    i = en.transpose(tp2.t[:, h * 128:(h + 1) * 128], stg.t[:, h * 128:(h + 1) * 128], ident_f[:])
                return i
            op("pe", [stg, "ident_f"], [tp2], f)
            op("dve", [tp2], ["wT"], lambda en, tp2=tp2: en.tensor_copy(out=wT_sb[:], in_=tp2.t[:]))
            rs = [ps.alloc(), ps.alloc()]

            def f(en, rs=rs):
                for h in range(4):
                    p0 = (h % 2) * 64
                    i = en.matmul(rs[h // 2].t[p0:p0 + 64, 0:128], lhsT=ones_b[:, 0:64], rhs=wT_sb[:, h * 128:(h + 1) * 128],
                                  start=True, stop=True)
                return i
            op("pe", ["wT", "ones"], rs, f)
            for pt in range(2):
                def f(en, pt=pt, rs=rs):
                    for rep in range(4):
                        i = en.scalar_tensor_tensor(out=BiasB[:, pt, rep * 128:(rep + 1) * 128], in0=rs[pt].t[:, 0:128],
                                                    scalar=pc(pt, 4), in1=BSt[:, pt, :], op0=ALU.mult, op1=ALU.add)
                    return i
                op("dve", [rs[pt], "pcm", "BSt"], ["biasB"], f)
            hk = [("hcat", kc) for kc in range(8)]
            for pt in range(2):
                kp = ps.alloc()

                def f(en, pt=pt, kp=kp):
                    for kc in range(8):
                        i = en.matmul(kp.t[:, 0:MEM], lhsT=hcat[:, kc, pt * 128:(pt + 1) * 128], rhs=memT[:, kc, :],
                                      start=(kc == 0), stop=(kc == 7))
                    return i
                op("pe", hk + ["memT"], [kp], f)
                op("dve", [kp], ["kT"], lambda en, pt=pt, kp=kp: en.tensor_copy(out=kT_sb[:, pt, :], in_=kp.t[:, 0:MEM]))
            for mc in range(2):
                vp = ps.alloc()

                def f(en, mc=mc, vp=vp):
                    for kc in range(8):
                        i = en.matmul(vp.t[:, 0:DG], lhsT=memT[:, kc, mc * 128:(mc + 1) * 128], rhs=hcat[:, kc, DG:2 * DG],
                                      start=(kc == 0), stop=(kc == 7))
                    return i
                op("pe", hk + ["memT"], [vp], f)
                op("dve", [vp], ["v"], lambda en, mc=mc, vp=vp: en.tensor_copy(out=v_sb[:, mc, :], in_=vp.t[:, 0:DG]))

            def load_x(b):
                for j in range(NT):
                    r0 = b * T + j * 128
                    dma("pool", [("xd", l, b, j)], [("x_bf", j)], x_bf[:, j, :], xsrc[r0:r0 + 128, :], ("xbf", j))

            chk("setupL")
            load_x(0)
            rcount = [0]
            for b in range(NBLK):
                slot, prev = b % 2, 1 - (b % 2)
                for (ext, nm, hl) in ((h_ext, "hext", 30), (xc_ext, "xc", 15), (m_ext, "mext", 2)):
                    wk = [(nm, slot, "halo")]
                    if b == 0:
                        op("pool", [], wk, lambda en, ext=ext, hl=hl: en.memset(ext[slot][:, :, 0:hl], 0.0))
                    else:
                        op("pool", [(nm, prev, 0), (nm, prev, 1)], wk,
                           lambda en, ext=ext, hl=hl: en.tensor_copy(out=ext[slot][:, :, 0:hl], in_=ext[prev][:, :, T:T + hl]))
                for kc in range(8):
                    tb = ps.alloc()
                    tbb = tb.t[:].bitcast(BF16)

                    def f(en, kc=kc, tbb=tbb):
                        for j in range(NT):
                            i = en.transpose(tbb[:, j * 128:(j + 1) * 128], x_bf[:, j, kc * 128:(kc + 1) * 128], ident_b[:])
                        return i
                    op("pe", [("x_bf", j) for j in range(NT)] + ["identb"], [tb], f)
                    if kc % 2 == 0:
                        op("act", [tb], [("xT", kc)], lambda en, kc=kc, tbb=tbb: en.copy(out=xT[:, kc, :], in_=tbb[:, 0:T]))
                    else:
                        op("dve", [tb], [("xT", kc)], lambda en, kc=kc, tbb=tbb: en.tensor_copy(out=xT[:, kc, :], in_=tbb[:, 0:T]))
                chk("xT")
                if b + 1 < NBLK:
                    load_x(b + 1)

                da = [proj(12), proj(13)]
                dg = [proj(14), proj(15)]
                for pt in range(2):
                    sig = rf.alloc()
                    op("act", [dg[pt]], [sig], lambda en, pt=pt, sig=sig: en.activation(
                        out=sig.t[:, 0:T], in_=dg[pt].t[:], func=AF.Sigmoid))
                    op("dve", [da[pt], sig], [("hext", slot, pt)], lambda en, pt=pt, sig=sig: en.tensor_tensor(
                        out=h_ext[slot][:, pt, 30:30 + T], in0=da[pt].t[:], in1=sig.t[:, 0:T], op=ALU.mult))
                chk("D1")
                for j in range(NT):
                    vps = ps.alloc()

                    def f(en, j=j, vps=vps):
                        for kc in range(8):
                            i = en.matmul(vps.t[:, 0:DG], lhsT=xT[:, kc, j * 128:(j + 1) * 128], rhs=w_in_sb[:, kc, 1024:1280],
                                          start=(kc == 0), stop=(kc == 7))
                        return i
                    op("pe", xT_keys + ["w_in"], [vps], f)
                    gv = rf.alloc()
                    op("act", [vps], [gv], lambda en, vps=vps, gv=gv: en.activation(
                        out=gv.t[:, 0:DG], in_=vps.t[:, 0:DG], func=AF.Gelu_apprx_tanh))
                    si = ln_stats(gv.t, 1, DG, [gv])
                    op("act", [gv, ("stv1", si), ("stv2", si)], [("vn", j)], lambda en, j=j, gv=gv, si=si: en.activation(
                        out=vn_tm[j][:], in_=gv.t[:, 0:DG], func=AF.Identity, scale=stv[si][:, 1:2], bias=stv[si][:, 2:3]))
                chk("Bv")
                hkeys = [("hext", slot, "halo"), ("hext", slot, 0), ("hext", slot, 1)]
                for j in range(NT):
                    cps = ps.alloc()

                    def f(en, j=j, cps=cps):
                        for pt in range(2):
                            for k in range(31):
                                en.matmul(cps.t[:, pt * 128:(pt + 1) * 128], lhsT=h_ext[slot][:, pt, j * 128 + k: j * 128 + k + 128],
                                          rhs=diag_sb[:, (pt * 31 + k) * 128:(pt * 31 + k + 1) * 128], start=(k == 0), stop=False)
                            i = en.matmul(cps.t[:, pt * 128:(pt + 1) * 128], lhsT=ones_row[0:1, 0:128],
                                          rhs=bdw_row[0:1, pt * 128:(pt + 1) * 128], start=False, stop=True)
                        return i
                    op("pe", hkeys + ["diag", "bdw", "ones_row"], [cps], f)
                    si = ln_stats(cps.t, 1, DG, [cps])
                    op("act", [cps, ("stv1", si), ("stv2", si)], [("hn", j)], lambda en, j=j, cps=cps, si=si: en.activation(
                        out=hn_tm[j][:], in_=cps.t[:, 0:DG], func=AF.Identity, scale=stv[si][:, 1:2], bias=stv[si][:, 2:3]))
                chk("Dconv")
                for pt in range(2):
                    ups = proj(6 + pt)
                    op("act", [ups], [("gu", pt)], lambda en, pt=pt, ups=ups: en.activation(
                        out=gelu_u[:, pt, :], in_=ups.t[:], func=AF.Gelu_apprx_tanh))
                mp = [ps.alloc(), ps.alloc()]

                def f(en, mp=mp):
                    for j in range(NT):
                        for h in range(4):
                            p0 = (h % 2) * 64
                            i = en.matmul(mp[h // 2].t[p0:p0 + 64, j * 128:(j + 1) * 128], lhsT=vn_tm[j][:, h * 64:(h + 1) * 64],
                                          rhs=wT_sb[:, h * 128:(h + 1) * 128], start=True, stop=True)
                    return i
                op("pe", [("vn", j) for j in range(NT)] + ["wT"], mp, f)
                chk("Bmix")
                tb = ps.alloc()
                tbb = tb.t[:].bitcast(BF16)

                def f(en, tbb=tbb):
                    for j in range(NT):
                        for pt in range(2):
                            i = en.transpose(tbb[:, pt * T + j * 128: pt * T + (j + 1) * 128], hn_tm[j][:, pt * 128:(pt + 1) * 128], ident_b[:])
                    return i
                op("pe", [("hn", j) for j in range(NT)] + ["identb"], [tb], f)
                sfm = []
                for pt in range(2):
                    s_ = rb.alloc()
                    op("act", [tb, "pcm"], [s_], lambda en, pt=pt, s_=s_, tbb=tbb: en.activation(
                        out=s_.t[:], in_=tbb[:, pt * T:(pt + 1) * T], func=AF.Silu, scale=pc(pt, 38), bias=pc(pt, 39)))
                    sfm.append(s_)
                for po in range(2):
                    pw = ps.alloc()

                    def f(en, po=po, pw=pw, sfm=sfm):
                        for kc in range(2):
                            i = en.matmul(pw.t[:], lhsT=w_pw_sb[:, kc, po * 128:(po + 1) * 128], rhs=sfm[kc].t[:],
                                          start=(kc == 0), stop=(kc == 1))
                        return i
                    op("pe", sfm + ["w_pw"], [pw], f)
                    sg = silu_gate(24 + po)
                    op("dve", [pw, sg], [("hcat", 6 + po)], lambda en, po=po, pw=pw, sg=sg: en.tensor_tensor(
                        out=hcat[:, 6 + po, :], in0=pw.t[:], in1=sg.t[:], op=ALU.mult))
                chk("Dtail")
                for pt in range(2):
                    sg = silu_gate(20 + pt)
                    op("dve", [sg, ("gu", pt)], [sg], lambda en, pt=pt, sg=sg: en.tensor_tensor(
                        out=sg.t[:], in0=sg.t[:], in1=gelu_u[:, pt, :], op=ALU.mult))
                    z = rf.alloc()
                    op("dve", [mp[pt], "pcm", "biasB"], [z], lambda en, pt=pt, z=z, mp=mp: en.scalar_tensor_tensor(
                        out=z.t[:, 0:T], in0=mp[pt].t[:], scalar=pc(pt, 3), in1=BiasB[:, pt, :], op0=ALU.mult, op1=ALU.add))
                    op("dve", [z, sg], [("hcat", 2 + pt)], lambda en, pt=pt, z=z, sg=sg: en.tensor_tensor(
                        out=hcat[:, 2 + pt, :], in0=z.t[:, 0:T], in1=sg.t[:], op=ALU.mult))
                chk("Bend")
                for pt in range(2):
                    qps = proj(16 + pt)
                    op("act", [qps], [("q", pt)], lambda en, pt=pt, qps=qps: en.copy(out=q_sb[:, pt, :], in_=qps.t[:]))
                for pair in range(2):
                    o_ps = ps.alloc()
                    d_ps = ps.alloc()
                    for hh in range(2):
                        h = pair * 2 + hh
                        p0 = hh * 64
                        pTs = []
                        for mc in range(2):
                            sps = ps.alloc()
                            op("pe", [("q", pair), "kT"], [sps], lambda en, mc=mc, sps=sps, p0=p0: en.matmul(
                                sps.t[:], lhsT=kT_sb[p0:p0 + 64, pair, mc * 128:(mc + 1) * 128], rhs=q_sb[p0:p0 + 64, pair, :],
                                start=True, stop=True))
                            pT = rb.alloc()
                            op("act", [sps], [pT], lambda en, sps=sps, pT=pT: en.activation(
                                out=pT.t[:], in_=sps.t[:], func=AF.Exp, scale=0.125))
                            pTs.append(pT)

                        def f(en, h=h, p0=p0, pTs=pTs, o_ps=o_ps, d_ps=d_ps):
                            for mc in range(2):
                                en.matmul(o_ps.t[p0:p0 + 64, :], lhsT=v_sb[:, mc, h * 64:(h + 1) * 64], rhs=pTs[mc].t[:],
                                          start=(mc == 0), stop=(mc == 1))
                            for mc in range(2):
                                i = en.matmul(d_ps.t[p0:p0 + 64, :], lhsT=ones_b[:, 0:64], rhs=pTs[mc].t[:],
                                              start=(mc == 0), stop=(mc == 1))
                            return i
                        op("pe", pTs + ["v", "ones"], [o_ps, d_ps], f)
                    rden = rf.alloc()
                    op("dve", [d_ps], [rden], lambda en, rden=rden, d_ps=d_ps: en.reciprocal(out=rden.t[:, 0:T], in_=d_ps.t[:]))
                    t_ = rf.alloc()
                    op("dve", [o_ps, rden], [t_], lambda en, t_=t_, o_ps=o_ps, rden=rden: en.tensor_tensor(
                        out=t_.t[:, 0:T], in0=o_ps.t[:], in1=rden.t[:, 0:T], op=ALU.mult))
                    sg = silu_gate(26 + pair)
                    op("dve", [t_, sg], [("hcat", 8 + pair)], lambda en, pair=pair, t_=t_, sg=sg: en.tensor_tensor(
                        out=hcat[:, 8 + pair, :], in0=t_.t[:, 0:T], in1=sg.t[:], op=ALU.mult))
                chk("E")
                X = xc_ext[slot]
                W = T + 15
                for pt in range(2):
                    xps = proj(10 + pt)
                    op("act", [xps], [("xc", slot, pt)], lambda en, pt=pt, xps=xps: en.copy(out=X[:, pt, 15:W], in_=xps.t[:]))
                xck = [("xc", slot, "halo"), ("xc", slot, 0), ("xc", slot, 1)]
                S2 = [rf.alloc(), rf.alloc()]
                S4 = [rf.alloc(), rf.alloc()]
                for pt in range(2):
                    op("dve", xck, [S2[pt]], lambda en, pt=pt: en.tensor_tensor(
                        out=S2[pt].t[:, 1:W], in0=X[:, pt, 1:W], in1=X[:, pt, 0:W - 1], op=ALU.add))
                op("dve", [S2[0]], [S4[0]], lambda en: en.tensor_tensor(
                    out=S4[0].t[64:128, 3:W], in0=S2[0].t[64:128, 3:W], in1=S2[0].t[64:128, 1:W - 2], op=ALU.add))
                op("dve", [S2[1]], [S4[1]], lambda en: en.tensor_tensor(
                    out=S4[1].t[:, 3:W], in0=S2[1].t[:, 3:W], in1=S2[1].t[:, 1:W - 2], op=ALU.add))
                op("dve", [S4[1]], [S2[1]], lambda en: en.tensor_tensor(
                    out=S2[1].t[:, 7:W], in0=S4[1].t[:, 7:W], in1=S4[1].t[:, 3:W - 4], op=ALU.add))
                op("dve", [S2[1]], [S4[1]], lambda en: en.tensor_tensor(
                    out=S4[1].t[64:128, 15:W], in0=S2[1].t[64:128, 15:W], in1=S2[1].t[64:128, 7:W - 8], op=ALU.add))
                yp = [rb.alloc(), rb.alloc()]
                srcs = [(0, 0, S2[0]), (0, 64, S4[0]), (1, 0, S2[1]), (1, 64, S4[1])]
                for (pt, p0, Sx) in srcs:
                    op("dve", [Sx, "invw"] + xck, [yp[pt]], lambda en, pt=pt, p0=p0, Sx=Sx: en.scalar_tensor_tensor(
                        out=yp[pt].t[p0:p0 + 64, :], in0=Sx.t[p0:p0 + 64, 15:W], scalar=invw[p0:p0 + 64, pt:pt + 1],
                        in1=X[p0:p0 + 64, pt, 15:W], op0=ALU.mult, op1=ALU.subtract))
                    if b == 0:
                        op("dve", [Sx, "invcnt"], ["ctmp"], lambda en, pt=pt, p0=p0, Sx=Sx: en.tensor_tensor(
                            out=ctmp[p0:p0 + 64, :], in0=Sx.t[p0:p0 + 64, 15:31], in1=invcnt[p0:p0 + 64, pt * 16:(pt + 1) * 16], op=ALU.mult))
                        op("dve", ["ctmp"] + xck, [yp[pt]], lambda en, pt=pt, p0=p0: en.tensor_tensor(
                            out=yp[pt].t[p0:p0 + 64, 0:16], in0=ctmp[p0:p0 + 64, :], in1=X[p0:p0 + 64, pt, 15:31], op=ALU.subtract))
                cp = [ps.alloc(), ps.alloc()]
                for g in range(4):
                    pt, p0 = g // 2, (g % 2) * 64
                    op("pe", [yp[pt], "poolw"], [cp[pt]], lambda en, pt=pt, p0=p0: en.matmul(
                        cp[pt].t[p0:p0 + 64, :], lhsT=pool_w_sb[p0:p0 + 64, pt, :], rhs=yp[pt].t[p0:p0 + 64, :], start=True, stop=True))
                for pt in range(2):
                    sg = silu_gate(22 + pt)
                    op("dve", [cp[pt], sg, "pcm"], [("hcat", 4 + pt)], lambda en, pt=pt, sg=sg, cp=cp: en.scalar_tensor_tensor(
                        out=hcat[:, 4 + pt, :], in0=cp[pt].t[:], scalar=pc(pt, 5), in1=sg.t[:], op0=ALU.mult, op1=ALU.mult))
                chk("C")
                M = m_ext[slot]
                xa = [proj(0), proj(1)]
                ca = [proj(4), proj(5)]
                for pt in range(2):
                    xs = rf.alloc()
                    op("act", [xa[pt]], [xs], lambda en, pt=pt, xs=xs: en.copy(out=xs.t[:, 0:T], in_=xa[pt].t[:]))
                    op("dve", [ca[pt], xs], [("mext", slot, pt)], lambda en, pt=pt, xs=xs: en.tensor_tensor(
                        out=M[:, pt, 2:2 + T], in0=ca[pt].t[:], in1=xs.t[:, 0:T], op=ALU.mult))
                mk = [("mext", slot, "halo"), ("mext", slot, 0), ("mext", slot, 1)]
                for pt in range(2):
                    acc = rf.alloc()
                    op("dve", mk + ["pcm"], [acc], lambda en, pt=pt, acc=acc: en.tensor_scalar(
                        out=acc.t[:, 0:T], in0=M[:, pt, 2:2 + T], scalar1=pc(pt, 2), scalar2=None, op0=ALU.mult))
                    op("dve", mk + ["pcm", acc], [acc], lambda en, pt=pt, acc=acc: en.scalar_tensor_tensor(
                        out=acc.t[:, 0:T], in0=M[:, pt, 1:1 + T], scalar=pc(pt, 1), in1=acc.t[:, 0:T], op0=ALU.mult, op1=ALU.add))
                    op("dve", mk + ["pcm", acc], [acc], lambda en, pt=pt, acc=acc: en.scalar_tensor_tensor(
                        out=acc.t[:, 0:T], in0=M[:, pt, 0:T], scalar=pc(pt, 0), in1=acc.t[:, 0:T], op0=ALU.mult, op1=ALU.add))
                    bps = proj(2 + pt)
                    op("dve", [bps, acc], [acc], lambda en, acc=acc, bps=bps: en.tensor_tensor(
                        out=acc.t[:, 0:T], in0=bps.t[:], in1=acc.t[:, 0:T], op=ALU.mult))
                    sg = silu_gate(18 + pt)
                    op("dve", [acc, sg], [("hcat", pt)], lambda en, pt=pt, acc=acc, sg=sg: en.tensor_tensor(
                        out=hcat[:, pt, :], in0=acc.t[:, 0:T], in1=sg.t[:], op=ALU.mult))
                chk("A")
                hck = [("hcat", kc) for kc in range(10)]
                for j in range(NT):
                    rs_ = rcount[0] % 2
                    rcount[0] += 1
                    r = r_buf[rs_]
                    rk = ("r", rs_)
                    r0 = b * T + j * 128
                    dma("sp", [("xd", l, b, j)], [rk], r[:], xsrc[r0:r0 + 128, :], ("rld", rs_))
                    yb = [ps.alloc(), ps.alloc()]

                    def f(en, j=j, yb=yb):
                        for hf in range(2):
                            for kc in range(10):
                                i = en.matmul(yb[hf].t[:], lhsT=hcat[:, kc, j * 128:(j + 1) * 128], rhs=w_out_sb[:, kc, hf * 512:(hf + 1) * 512],
                                              start=(kc == 0), stop=(kc == 9))
                        return i
                    op("pe", hck + ["w_out"], yb, f)

                    def f(en, r=r, yb=yb):
                        for hf in range(2):
                            i = en.scalar_tensor_tensor(out=r[:, hf * 512:(hf + 1) * 512], in0=r[:, hf * 512:(hf + 1) * 512], scalar=ALPHA,
                                                        in1=yb[hf].t[:], op0=ALU.mult, op1=ALU.add)
                        return i
                    op("dve", [rk] + yb, [rk], f)
                    si = ln_stats(r, 2, 512, [rk])
                    op("act", [rk, ("stv1", si), ("stv2", si)], [rk], lambda en, r=r, si=si: en.activation(
                        out=r[:], in_=r[:], func=AF.Identity, scale=stv[si][:, 1:2], bias=stv[si][:, 2:3]))
                    op("pool", [rk, "lng"], [rk], lambda en, r=r: en.tensor_tensor(out=r[:], in0=r[:], in1=lng_bc[:], op=ALU.mult))
                    op("pool", [rk, "lnb"], [rk], lambda en, r=r: en.tensor_tensor(out=r[:], in0=r[:], in1=lnb_bc[:], op=ALU.add))
                    dma("sp", [rk], [("xd", l + 1, b, j)], xdst[r0:r0 + 128, :], r[:], ("rst", rs_))
                chk("blk%d" % b)
        except StopBuild:
            pass
        for e in tr.engs:
            if tr.cnt[e] > 0:
                tr._wait("sp", (tr.sem[e], tr.cnt[e], e))
        for k, s in tr.dsem.items():
            tr._wait("sp", (s[0], s[1], "dma:" + str(k)))
    return nc


def _consts():
    ident = np.eye(128, dtype=np.float32)
    mask = np.tril(np.ones((128, 128), dtype=np.float32))
    wins = [2, 4, 8, 16]
    invw = np.zeros((128, 2), np.float32)
    invcnt = np.zeros((128, 32), np.float32)
    for pt in range(2):
        for half in range(2):
            w = wins[pt * 2 + half]
            invw[half * 64:(half + 1) * 64, pt] = 1.0 / w
            for t in range(16):
                invcnt[half * 64:(half + 1) * 64, pt * 16 + t] = 1.0 / min(t + 1, w)
    return {"c_ident": ident, "c_mask": mask, "c_invw": invw, "c_invcnt": invcnt}


_NC_CACHE = {}
FUSED = True


def kernel(**inputs):
    x = np.ascontiguousarray(inputs["x"], dtype=np.float32)
    mem = np.ascontiguousarray(inputs["mem"], dtype=np.float32)
    consts = _consts()
    L = inputs["w_in"].shape[0]
    if FUSED:
        if L not in _NC_CACHE:
            _NC_CACHE[L] = build(L)
        nc = _NC_CACHE[L]
        in_maps = []
        for c in range(N_CORES):
            m = {"x": x[c], "mem": mem[c]}
            for p in PARAMS:
                m[p] = np.ascontiguousarray(inputs[p], dtype=np.float32)
            m.update(consts)
            in_maps.append(m)
        res = run_bass_kernel_spmd(nc, in_maps, core_ids=list(range(N_CORES)))
        return np.stack([np.asarray(r["y"]) for r in res.results], axis=0).astype(np.float32)
    if 1 not in _NC_CACHE:
        _NC_CACHE[1] = build(1)
    nc = _NC_CACHE[1]
    cur = x
    for l in range(L):
        in_maps = []
        for c in range(N_CORES):
            m = {"x": np.ascontiguousarray(cur[c]), "mem": mem[c]}
            for p in PARAMS:
                m[p] = np.ascontiguousarray(inputs[p][l:l + 1], dtype=np.float32)
            m.update(consts)
            in_maps.append(m)
        res = run_bass_kernel_spmd(nc, in_maps, core_ids=list(range(N_CORES)))
        cur = np.stack([np.asarray(r["y"]) for r in res.results], axis=0).astype(np.float32)
    return cur
```

```python
import numpy as np
from contextlib import ExitStack
import concourse.bass as bass
import concourse.mybir as mybir
from concourse.bass_utils import run_bass_kernel_spmd

F32, BF16 = mybir.dt.float32, mybir.dt.bfloat16
AF = mybir.ActivationFunctionType
ALU = mybir.AluOpType

D = 1024; S = 4096; DG = 256; DIN = 3584; DMIX = 1280; MEM = 256
T = 512; NT = 4; NBLK = S // T
ALPHA = float((2.0 * 2) ** 0.25)
EPS = 1e-5
N_CORES = 8
PARAMS = ["w_in", "conv_a_w", "sg_ln_g", "sg_ln_b", "sg_w", "sg_b", "pool_w", "pool_scale",
          "cc_dw_w", "cc_dw_b", "cc_ln_g", "cc_ln_b", "cc_pw_w", "w_kv", "w_out", "ln_g", "ln_b"]
PSHAPES = {"w_in": [D, DIN], "conv_a_w": [3, DG], "sg_ln_g": [DG], "sg_ln_b": [DG], "sg_w": [4, 128, 128],
           "sg_b": [4, 128], "pool_w": [4, 64, 64], "pool_scale": [DG], "cc_dw_w": [31, DG], "cc_dw_b": [DG],
           "cc_ln_g": [DG], "cc_ln_b": [DG], "cc_pw_w": [DG, DG], "w_kv": [D, 2 * DG], "w_out": [DMIX, D],
           "ln_g": [D], "ln_b": [D]}


class H:
    __slots__ = ("t", "key", "ring", "slot", "gen")

    def __init__(self, t, key, ring=None, slot=0, gen=0):
        self.t, self.key, self.ring, self.slot, self.gen = t, key, ring, slot, gen


class Ring:
    def __init__(self, name, tensors):
        self.name, self.tensors, self.i = name, tensors, 0
        self.gen = [0] * len(tensors)

    def alloc(self):
        s = self.i % len(self.tensors)
        self.i += 1
        self.gen[s] += 1
        return H(self.tensors[s], (self.name, s), self, s, self.gen[s])


class Tr:
    def __init__(self, nc, es):
        self.nc, self.es = nc, es
        self.engs = {"pe": nc.tensor, "act": nc.scalar, "dve": nc.vector, "pool": nc.gpsimd, "sp": nc.sync}
        self.sem = {e: es.enter_context(nc.semaphore("s_" + e)) for e in self.engs}
        self.cnt = {e: 0 for e in self.engs}
        self.waited = {e: {} for e in self.engs}
        self.lastw = {}
        self.readers = {}
        self.dsem = {}

    def _key(self, b):
        if isinstance(b, H):
            assert b.ring is None or b.ring.gen[b.slot] == b.gen, f"ring buffer {b.key} reused while live"
            return b.key
        return b

    def _wait(self, e, tok):
        sem, val, sid = tok
        if self.waited[e].get(sid, 0) >= val:
            return
        self.engs[e].wait_ge(sem, val)
        self.waited[e][sid] = val

    def op(self, e, reads, writes, fn, dma=None):
        rk = [self._key(b) for b in reads]
        wk = [self._key(b) for b in writes]
        deps = []
        for k in rk:
            if k in self.lastw:
                deps.append(self.lastw[k])
        for k in wk:
            if k in self.lastw:
                deps.append(self.lastw[k])
            deps.extend(self.readers.get(k, {}).values())
        for tok in deps:
            if e == "pe" and tok[2] == "pe":
                continue
            self._wait(e, tok)
        inst = fn(self.engs[e])
        if dma is None:
            self.cnt[e] += 1
            inst.then_inc(self.sem[e], 1)
            tok = (self.sem[e], self.cnt[e], e)
        else:
            if dma not in self.dsem:
                self.dsem[dma] = [self.es.enter_context(self.nc.semaphore("d_%d" % len(self.dsem))), 0]
            s = self.dsem[dma]
            s[1] += 16
            inst.then_inc(s[0], 16)
            tok = (s[0], s[1], "dma:" + str(dma))
        for k in rk:
            d = self.readers.setdefault(k, {})
            old = d.get(tok[2])
            if old is None or old[1] < tok[1]:
                d[tok[2]] = tok
        for k in wk:
            self.lastw[k] = tok
            self.readers[k] = {}
        return tok


class StopBuild(Exception):
    pass


STOP = None


def chk(name):
    if STOP == name:
        raise StopBuild()


def build(nl):
    nc = bass.Bass("TRN2", target_bir_lowering=False)
    dr = {}
    dr["x"] = nc.dram_tensor("x", [S, D], F32, kind="ExternalInput").ap()
    dr["mem"] = nc.dram_tensor("mem", [MEM, D], F32, kind="ExternalInput").ap()
    for p in PARAMS:
        dr[p] = nc.dram_tensor(p, [nl] + PSHAPES[p], F32, kind="ExternalInput").ap()
    dr["c_ident"] = nc.dram_tensor("c_ident", [128, 128], F32, kind="ExternalInput").ap()
    dr["c_mask"] = nc.dram_tensor("c_mask", [128, 128], F32, kind="ExternalInput").ap()
    dr["c_invw"] = nc.dram_tensor("c_invw", [128, 2], F32, kind="ExternalInput").ap()
    dr["c_invcnt"] = nc.dram_tensor("c_invcnt", [128, 32], F32, kind="ExternalInput").ap()
    y_out = nc.dram_tensor("y", [S, D], F32, kind="ExternalOutput").ap()
    xscr = [nc.dram_tensor("xscr%d" % i, [S, D], F32, kind="Internal").ap() for i in range(nl - 1)]

    es = ExitStack()
    with es:
        def sb(name, shape, dt):
            return es.enter_context(nc.sbuf_tensor(name, shape, dt))

        tr = Tr(nc, es)
        op = tr.op
        ident_f = sb("ident_f", [128, 128], F32); ident_b = sb("ident_b", [128, 128], BF16)
        mask_f = sb("mask_f", [128, 128], F32)
        invw = sb("invw", [128, 2], F32); invcnt = sb("invcnt", [128, 32], F32)
        ones_b = sb("ones_b", [128, 128], BF16); neghalf = sb("neghalf", [128, 4], F32)
        ones_row = sb("ones_row", [1, 128], BF16)
        w_in_sb = sb("w_in_sb", [128, 8, DIN], BF16)
        w_out_sb = sb("w_out_sb", [128, 10, D], BF16)
        w_pw_sb = sb("w_pw_sb", [128, 2, DG], BF16)
        pool_w_sb = sb("pool_w_sb", [128, 2, 64], BF16)
        PR = sb("PR", [40, DG], F32)
        pcm = sb("pcm", [128, 80], F32)
        bdw_row = sb("bdw_row", [1, DG], BF16)
        lng_bc = sb("lng_bc", [128, D], F32); lnb_bc = sb("lnb_bc", [128, D], F32)
        wT_sb = sb("wT_sb", [128, 512], BF16)
        BSt = sb("BSt", [128, 2, 128], F32)
        BiasB = sb("BiasB", [128, 2, 512], F32)
        diag_sb = sb("diag_sb", [128, 62 * 128], BF16)
        memT = sb("memT", [128, 8, MEM], BF16)
        kT_sb = sb("kT_sb", [128, 2, MEM], BF16)
        v_sb = sb("v_sb", [128, 2, DG], BF16)
        x_bf = sb("x_bf", [128, NT, D], BF16)
        xT = sb("xT", [128, 8, T], BF16)
        hcat = sb("hcat", [128, 10, T], BF16)
        m_ext = [sb("m_ext%d" % i, [128, 2, T + 2], F32) for i in range(2)]
        xc_ext = [sb("xc_ext%d" % i, [128, 2, T + 15], F32) for i in range(2)]
        h_ext = [sb("h_ext%d" % i, [128, 2, T + 30], BF16) for i in range(2)]
        r_buf = [sb("r_buf%d" % i, [128, D], F32) for i in range(2)]
        vn_tm = [sb("vn_tm%d" % i, [128, DG], BF16) for i in range(NT)]
        hn_tm = [sb("hn_tm%d" % i, [128, DG], BF16) for i in range(NT)]
        gelu_u = sb("gelu_u", [128, 2, T], BF16)
        q_sb = sb("q_sb", [128, 2, T], BF16)
        ctmp = sb("ctmp", [128, 16], F32)
        acc_buf = [sb("acc_buf%d" % i, [128, T], F32) for i in range(2)]
        yp_buf = [sb("yp_buf%d" % i, [128, T], BF16) for i in range(2)]
        NST = 4
        st6 = [sb("st6_%d" % i, [128, 12], F32) for i in range(NST)]
        stmv = [sb("stmv_%d" % i, [128, 2], F32) for i in range(NST)]
        stv = [sb("stv_%d" % i, [128, 4], F32) for i in range(NST)]
        stc = [0]
        rf = Ring("rf", [sb("rf%d" % i, [128, 544], F32) for i in range(7)])
        rb = Ring("rb", [sb("rb%d" % i, [128, T], BF16) for i in range(6)])
        ps = Ring("ps", [es.enter_context(nc.psum_tensor("ps%d" % i, [128, 512], F32)) for i in range(8)])

        def pc(pt, r):
            return pcm[:, pt * 40 + r: pt * 40 + r + 1]

        def dma(e, reads, writes, out, in_, key, **kw):
            return op(e, reads, writes, lambda en: en.dma_start(out=out, in_=in_, **kw), dma=key)

        dma("sp", [], ["ident_f"], ident_f[:], dr["c_ident"], "consts0")
        dma("sp", [], ["mask"], mask_f[:], dr["c_mask"], "consts1")
        dma("sp", [], ["invw"], invw[:], dr["c_invw"], "consts2")
        dma("sp", [], ["invcnt"], invcnt[:], dr["c_invcnt"], "consts3")
        op("dve", ["ident_f"], ["identb"], lambda en: en.tensor_copy(out=ident_b[:], in_=ident_f[:]))
        op("dve", [], ["ones"], lambda en: en.memset(ones_b[:], 1.0))
        op("dve", [], ["ones_row"], lambda en: en.memset(ones_row[:], 1.0))
        op("pool", [], ["neghalf"], lambda en: en.memset(neghalf[:], -0.5))
        for mc in range(2):
            dma("pool", [], [("x_bf", mc)], x_bf[:, mc, :], dr["mem"][mc * 128:(mc + 1) * 128, :], ("xbf", mc))
        for kc in range(8):
            tb = ps.alloc()
            tbb = tb.t[:].bitcast(BF16)

            def f(en, kc=kc, tbb=tbb):
                for mc in range(2):
                    i = en.transpose(tbb[:, mc * 128:(mc + 1) * 128], x_bf[:, mc, kc * 128:(kc + 1) * 128], ident_b[:])
                return i
            op("pe", [("x_bf", 0), ("x_bf", 1), "identb"], [tb], f)
            op("dve", [tb], ["memT"], lambda en, kc=kc, tbb=tbb: en.tensor_copy(out=memT[:, kc, :], in_=tbb[:, 0:MEM]))

        def ln_a(src_ap, nchunk, csz, src_keys):
            si = stc[0] % NST
            stc[0] += 1
            mv, sv = stmv[si], stv[si]

            def f(en):
                for c in range(nchunk):
                    i = en.bn_stats(out=st6[si][:, c * 6:(c + 1) * 6], in_=src_ap[:, c * csz:(c + 1) * csz])
                return i
            op("dve", src_keys, [("st6", si)], f)
            op("dve", [("st6", si)], [("stmv", si)], lambda en: en.bn_aggr(out=mv[:], in_=st6[si][:, 0:6 * nchunk]))
            op("dve", [("stmv", si)], [("stv0", si)], lambda en: en.tensor_scalar(
                out=sv[:, 0:1], in0=mv[:, 1:2], scalar1=EPS, scalar2=None, op0=ALU.add))
            op("pool", [("stv0", si), "neghalf"], [("stv1", si)], lambda en: en.tensor_tensor(
                out=sv[:, 1:2], in0=sv[:, 0:1], in1=neghalf[:, 0:1], op=ALU.pow))
            return si

        def ln_b(si):
            mv, sv = stmv[si], stv[si]
            op("dve", [("stmv", si), ("stv1", si)], [("stv2", si)], lambda en: en.scalar_tensor_tensor(
                out=sv[:, 2:3], in0=mv[:, 0:1], scalar=-1.0, in1=sv[:, 1:2], op0=ALU.mult, op1=ALU.mult))

        xT_keys = [("xT", kc) for kc in range(8)]

        def proj(col):
            bank = ps.alloc()

            def f(en):
                for kc in range(8):
                    i = en.matmul(bank.t[:], lhsT=w_in_sb[:, kc, col * 128:(col + 1) * 128], rhs=xT[:, kc, :],
                                  start=(kc == 0), stop=(kc == 7))
                return i
            op("pe", xT_keys + ["w_in"], [bank], f)
            return bank

        def silu_gate(col):
            g = proj(col)
            sg = rb.alloc()
            op("act", [g], [sg], lambda en: en.activation(out=sg.t[:], in_=g.t[:], func=AF.Silu))
            return sg

        try:
          chk("setup0")
          for l in range(nl):
            xsrc = dr["x"] if l == 0 else xscr[l - 1]
            xdst = y_out if l == nl - 1 else xscr[l]

            for kc in range(8):
                for hf in range(2):
                    dma("pool", [], ["w_in"], w_in_sb[:, kc, hf * 1792:(hf + 1) * 1792],
                        dr["w_in"][l][kc * 128:(kc + 1) * 128, hf * 1792:(hf + 1) * 1792], "w_in")
            for kc in range(8):
                dma("pool", [], [("hcat", kc)], hcat[:, kc, :], dr["w_kv"][l][kc * 128:(kc + 1) * 128, :], "w_kv")
            for kc in range(2):
                dma("pool", [], ["w_pw"], w_pw_sb[:, kc, :], dr["cc_pw_w"][l][kc * 128:(kc + 1) * 128, :], "w_pw")
            for g in range(4):
                p0 = (g % 2) * 64
                dma("pool", [], ["poolw"], pool_w_sb[p0:p0 + 64, g // 2, :], dr["pool_w"][l][g], "poolw")
            dma("pool", [], ["bdw"], bdw_row[0:1, :], dr["cc_dw_b"][l:l + 1, :], "bdw")
            for kc in range(10):
                dma("pool", [], ["w_out"], w_out_sb[:, kc, :], dr["w_out"][l][kc * 128:(kc + 1) * 128, :], "w_out")
            dma("sp", [], ["PR"], PR[0:3, :], dr["conv_a_w"][l], "PR")
            dma("sp", [], ["PR"], PR[3:4, :], dr["sg_ln_g"][l:l + 1, :], "PR")
            dma("sp", [], ["PR"], PR[4:5, :], dr["sg_ln_b"][l:l + 1, :], "PR")
            dma("sp", [], ["PR"], PR[5:6, :], dr["pool_scale"][l:l + 1, :], "PR")
            dma("sp", [], ["PR"], PR[6:37, :], dr["cc_dw_w"][l], "PR")
            dma("sp", [], ["PR"], PR[37:38, :], dr["cc_dw_b"][l:l + 1, :], "PR")
            dma("sp", [], ["PR"], PR[38:39, :], dr["cc_ln_g"][l:l + 1, :], "PR")
            dma("sp", [], ["PR"], PR[39:40, :], dr["cc_ln_b"][l:l + 1, :], "PR")
            dma("sp", [], ["lng"], lng_bc[:], dr["ln_g"][l].partition_broadcast(128), "lng")
            dma("sp", [], ["lnb"], lnb_bc[:], dr["ln_b"][l].partition_broadcast(128), "lnb")
            for h in range(4):
                p0 = (h % 2) * 64
                dma("sp", [], ["BSt"], BSt[p0:p0 + 64, h // 2, :], dr["sg_b"][l][h].partition_broadcast(64), "BSt")
            stg = rf.alloc()
            for h in range(4):
                dma("sp", [], [stg], stg.t[:, h * 128:(h + 1) * 128], dr["sg_w"][l][h], "sgw")

            tp = ps.alloc()

            def f(en, tp=tp):
                for pt in range(2):
                    i = en.transpose(tp.t[:, pt * 40:(pt + 1) * 40], PR[0:40, pt * 128:(pt + 1) * 128], ident_f[0:40, 0:40])
                return i
            op("pe", ["PR", "ident_f"], [tp], f)
            op("dve", [tp], ["pcm"], lambda en, tp=tp: en.tensor_copy(out=pcm[:, 0:80], in_=tp.t[:, 0:80]))

            def f(en):
                for pt in range(2):
                    for k in range(31):
                        j = pt * 31 + k
                        i = en.tensor_scalar(out=diag_sb[:, j * 128:(j + 1) * 128], in0=ident_b[:],
                                             scalar1=pc(pt, 6 + k), scalar2=None, op0=ALU.mult)
                return i
            op("dve", ["pcm", "identb"], ["diag"], f)

            def f(en, stg=stg):
                for h in range(4):
                    i = en.tensor_tensor(out=stg.t[:, h * 128:(h + 1) * 128], in0=stg.t[:, h * 128:(h + 1) * 128],
                                         in1=mask_f[:], op=ALU.mult)
                return i
            op("dve", [stg, "mask"], [stg], f)
            tp2 = ps.alloc()

            def f(en, stg=stg, tp2=tp2):
                for h in range(4):
                    i = en.transpose(tp2.t[:, h * 128:(h + 1) * 128], stg.t[:, h * 128:(h + 1) * 128], ident_f[:])
                return i
            op("pe", [stg, "ident_f"], [tp2], f)
            op("dve", [tp2], ["wT"], lambda en, tp2=tp2: en.tensor_copy(out=wT_sb[:], in_=tp2.t[:]))
            rs = [ps.alloc(), ps.alloc()]

            def f(en, rs=rs):
                for h in range(4):
                    p0 = (h % 2) * 64
                    i = en.matmul(rs[h // 2].t[p0:p0 + 64, 0:128], lhsT=ones_b[:, 0:64], rhs=wT_sb[:, h * 128:(h + 1) * 128],
                                  start=True, stop=True)
                return i
            op("pe", ["wT", "ones"], rs, f)
            for pt in range(2):
                def f(en, pt=pt, rs=rs):
                    for rep in range(4):
                        i = en.scalar_tensor_tensor(out=BiasB[:, pt, rep * 128:(rep + 1) * 128], in0=rs[pt].t[:, 0:128],
                                                    scalar=pc(pt, 4), in1=BSt[:, pt, :], op0=ALU.mult, op1=ALU.add)
                    return i
                op("dve", [rs[pt], "pcm", "BSt"], ["biasB"], f)
            hk = [("hcat", kc) for kc in range(8)]
            for pt in range(2):
                kp = ps.alloc()

                def f(en, pt=pt, kp=kp):
                    for kc in range(8):
                        i = en.matmul(kp.t[:, 0:MEM], lhsT=hcat[:, kc, pt * 128:(pt + 1) * 128], rhs=memT[:, kc, :],
                                      start=(kc == 0), stop=(kc == 7))
                    return i
                op("pe", hk + ["memT"], [kp], f)
                op("dve", [kp], ["kT"], lambda en, pt=pt, kp=kp: en.tensor_copy(out=kT_sb[:, pt, :], in_=kp.t[:, 0:MEM]))
            for mc in range(2):
                vp = ps.alloc()

                def f(en, mc=mc, vp=vp):
                    for kc in range(8):
                        i = en.matmul(vp.t[:, 0:DG], lhsT=memT[:, kc, mc * 128:(mc + 1) * 128], rhs=hcat[:, kc, DG:2 * DG],
                                      start=(kc == 0), stop=(kc == 7))
                    return i
                op("pe", hk + ["memT"], [vp], f)
                op("dve", [vp], ["v"], lambda en, mc=mc, vp=vp: en.tensor_copy(out=v_sb[:, mc, :], in_=vp.t[:, 0:DG]))

            def load_x(b):
                for j in range(NT):
                    r0 = b * T + j * 128
                    dma("pool", [("xd", l, b, j)], [("x_bf", j)], x_bf[:, j, :], xsrc[r0:r0 + 128, :], ("xbf", j))

            chk("setupL")
            load_x(0)

            def emit_xT():
                for kc in range(8):
                    tb = ps.alloc()
                    tbb = tb.t[:].bitcast(BF16)

                    def f(en, kc=kc, tbb=tbb):
                        for j in range(NT):
                            i = en.transpose(tbb[:, j * 128:(j + 1) * 128], x_bf[:, j, kc * 128:(kc + 1) * 128], ident_b[:])
                        return i
                    op("pe", [("x_bf", j) for j in range(NT)] + ["identb"], [tb], f)
                    if kc % 2 == 0:
                        op("act", [tb], [("xT", kc)], lambda en, kc=kc, tbb=tbb: en.copy(out=xT[:, kc, :], in_=tbb[:, 0:T]))
                    else:
                        op("dve", [tb], [("xT", kc)], lambda en, kc=kc, tbb=tbb: en.tensor_copy(out=xT[:, kc, :], in_=tbb[:, 0:T]))

            emit_xT()
            load_x(1)
            rcount = [0]
            for b in range(NBLK):
                slot, prev = b % 2, 1 - (b % 2)
                for (ext, nm, hl) in ((h_ext, "hext", 30), (xc_ext, "xc", 15), (m_ext, "mext", 2)):
                    wk = [(nm, slot, "halo")]
                    if b == 0:
                        op("pool", [], wk, lambda en, ext=ext, hl=hl: en.memset(ext[slot][:, :, 0:hl], 0.0))
                    else:
                        op("pool", [(nm, prev, 0), (nm, prev, 1)], wk,
                           lambda en, ext=ext, hl=hl: en.tensor_copy(out=ext[slot][:, :, 0:hl], in_=ext[prev][:, :, T:T + hl]))
                da = [proj(12), proj(13)]
                dg = [proj(14), proj(15)]
                for pt in range(2):
                    sig = rf.alloc()
                    op("act", [dg[pt]], [sig], lambda en, pt=pt, sig=sig: en.activation(
                        out=sig.t[:, 0:T], in_=dg[pt].t[:], func=AF.Sigmoid))
                    op("dve", [da[pt], sig], [("hext", slot, pt)], lambda en, pt=pt, sig=sig: en.tensor_tensor(
                        out=h_ext[slot][:, pt, 30:30 + T], in0=da[pt].t[:], in1=sig.t[:, 0:T], op=ALU.mult))
                pend = []
                for j in range(NT):
                    vps = ps.alloc()

                    def f(en, j=j, vps=vps):
                        for kc in range(8):
                            i = en.matmul(vps.t[:, 0:DG], lhsT=xT[:, kc, j * 128:(j + 1) * 128], rhs=w_in_sb[:, kc, 1024:1280],
                                          start=(kc == 0), stop=(kc == 7))
                        return i
                    op("pe", xT_keys + ["w_in"], [vps], f)
                    gv = rf.alloc()
                    op("act", [vps], [gv], lambda en, vps=vps, gv=gv: en.activation(
                        out=gv.t[:, 0:DG], in_=vps.t[:, 0:DG], func=AF.Gelu_apprx_tanh))
                    si = ln_a(gv.t, 1, DG, [gv])
                    pend.append((j, gv, si))
                    if len(pend) == 2 or j == NT - 1:
                        for _ in range(1 if j < NT - 1 else len(pend)):
                            (j0, gv0, si0) = pend.pop(0)
                            ln_b(si0)
                            op("act", [gv0, ("stv1", si0), ("stv2", si0)], [("vn", j0)], lambda en, j0=j0, gv0=gv0, si0=si0: en.activation(
                                out=vn_tm[j0][:], in_=gv0.t[:, 0:DG], func=AF.Identity, scale=stv[si0][:, 1:2], bias=stv[si0][:, 2:3]))
                X = xc_ext[slot]
                W = T + 15
                for pt in range(2):
                    xps = proj(10 + pt)
                    op("act", [xps], [("xc", slot, pt)], lambda en, pt=pt, xps=xps: en.copy(out=X[:, pt, 15:W], in_=xps.t[:]))
                xck = [("xc", slot, "halo"), ("xc", slot, 0), ("xc", slot, 1)]
                S2 = [rf.alloc(), rf.alloc()]
                S4 = [rf.alloc(), rf.alloc()]
                for pt in range(2):
                    op("dve", xck, [S2[pt]], lambda en, pt=pt: en.tensor_tensor(
                        out=S2[pt].t[:, 1:W], in0=X[:, pt, 1:W], in1=X[:, pt, 0:W - 1], op=ALU.add))
                op("dve", [S2[0]], [S4[0]], lambda en: en.tensor_tensor(
                    out=S4[0].t[64:128, 3:W], in0=S2[0].t[64:128, 3:W], in1=S2[0].t[64:128, 1:W - 2], op=ALU.add))
                op("dve", [S2[1]], [S4[1]], lambda en: en.tensor_tensor(
                    out=S4[1].t[:, 3:W], in0=S2[1].t[:, 3:W], in1=S2[1].t[:, 1:W - 2], op=ALU.add))
                op("dve", [S4[1]], [S2[1]], lambda en: en.tensor_tensor(
                    out=S2[1].t[:, 7:W], in0=S4[1].t[:, 7:W], in1=S4[1].t[:, 3:W - 4], op=ALU.add))
                op("dve", [S2[1]], [S4[1]], lambda en: en.tensor_tensor(
                    out=S4[1].t[64:128, 15:W], in0=S2[1].t[64:128, 15:W], in1=S2[1].t[64:128, 7:W - 8], op=ALU.add))
                yp = [H(yp_buf[0], ("yp", 0)), H(yp_buf[1], ("yp", 1))]
                srcs = [(0, 0, S2[0]), (0, 64, S4[0]), (1, 0, S2[1]), (1, 64, S4[1])]
                for (pt, p0, Sx) in srcs:
                    op("dve", [Sx, "invw"] + xck, [yp[pt]], lambda en, pt=pt, p0=p0, Sx=Sx: en.scalar_tensor_tensor(
                        out=yp[pt].t[p0:p0 + 64, :], in0=Sx.t[p0:p0 + 64, 15:W], scalar=invw[p0:p0 + 64, pt:pt + 1],
                        in1=X[p0:p0 + 64, pt, 15:W], op0=ALU.mult, op1=ALU.subtract))
                    if b == 0:
                        op("dve", [Sx, "invcnt"], ["ctmp"], lambda en, pt=pt, p0=p0, Sx=Sx: en.tensor_tensor(
                            out=ctmp[p0:p0 + 64, :], in0=Sx.t[p0:p0 + 64, 15:31], in1=invcnt[p0:p0 + 64, pt * 16:(pt + 1) * 16], op=ALU.mult))
                        op("dve", ["ctmp"] + xck, [yp[pt]], lambda en, pt=pt, p0=p0: en.tensor_tensor(
                            out=yp[pt].t[p0:p0 + 64, 0:16], in0=ctmp[p0:p0 + 64, :], in1=X[p0:p0 + 64, pt, 15:31], op=ALU.subtract))
                M = m_ext[slot]
                xa = [proj(0), proj(1)]
                ca = [proj(4), proj(5)]
                for pt in range(2):
                    xs = rf.alloc()
                    op("act", [xa[pt]], [xs], lambda en, pt=pt, xs=xs: en.copy(out=xs.t[:, 0:T], in_=xa[pt].t[:]))
                    op("dve", [ca[pt], xs], [("mext", slot, pt)], lambda en, pt=pt, xs=xs: en.tensor_tensor(
                        out=M[:, pt, 2:2 + T], in0=ca[pt].t[:], in1=xs.t[:, 0:T], op=ALU.mult))
                mk = [("mext", slot, "halo"), ("mext", slot, 0), ("mext", slot, 1)]
                accs = []
                for pt in range(2):
                    acc = H(acc_buf[pt], ("acc", pt))
                    accs.append(acc)
                    op("dve", mk + ["pcm"], [acc], lambda en, pt=pt, acc=acc: en.tensor_scalar(
                        out=acc.t[:, 0:T], in0=M[:, pt, 2:2 + T], scalar1=pc(pt, 2), scalar2=None, op0=ALU.mult))
                    op("dve", mk + ["pcm", acc], [acc], lambda en, pt=pt, acc=acc: en.scalar_tensor_tensor(
                        out=acc.t[:, 0:T], in0=M[:, pt, 1:1 + T], scalar=pc(pt, 1), in1=acc.t[:, 0:T], op0=ALU.mult, op1=ALU.add))
                    op("dve", mk + ["pcm", acc], [acc], lambda en, pt=pt, acc=acc: en.scalar_tensor_tensor(
                        out=acc.t[:, 0:T], in0=M[:, pt, 0:T], scalar=pc(pt, 0), in1=acc.t[:, 0:T], op0=ALU.mult, op1=ALU.add))
                hkeys = [("hext", slot, "halo"), ("hext", slot, 0), ("hext", slot, 1)]
                pend = []
                for j in range(NT):
                    cps = ps.alloc()

                    def f(en, j=j, cps=cps):
                        for pt in range(2):
                            for k in range(31):
                                en.matmul(cps.t[:, pt * 128:(pt + 1) * 128], lhsT=h_ext[slot][:, pt, j * 128 + k: j * 128 + k + 128],
                                          rhs=diag_sb[:, (pt * 31 + k) * 128:(pt * 31 + k + 1) * 128], start=(k == 0), stop=False)
                            i = en.matmul(cps.t[:, pt * 128:(pt + 1) * 128], lhsT=ones_row[0:1, 0:128],
                                          rhs=bdw_row[0:1, pt * 128:(pt + 1) * 128], start=False, stop=True)
                        return i
                    op("pe", hkeys + ["diag", "bdw", "ones_row"], [cps], f)
                    si = ln_a(cps.t, 1, DG, [cps])
                    pend.append((j, cps, si))
                    if len(pend) == 2 or j == NT - 1:
                        for _ in range(1 if j < NT - 1 else len(pend)):
                            (j0, cps0, si0) = pend.pop(0)
                            ln_b(si0)
                            op("act", [cps0, ("stv1", si0), ("stv2", si0)], [("hn", j0)], lambda en, j0=j0, cps0=cps0, si0=si0: en.activation(
                                out=hn_tm[j0][:], in_=cps0.t[:, 0:DG], func=AF.Identity, scale=stv[si0][:, 1:2], bias=stv[si0][:, 2:3]))
                for pt in range(2):
                    ups = proj(6 + pt)
                    op("act", [ups], [("gu", pt)], lambda en, pt=pt, ups=ups: en.activation(
                        out=gelu_u[:, pt, :], in_=ups.t[:], func=AF.Gelu_apprx_tanh))
                for pt in range(2):
                    qps = proj(16 + pt)
                    op("act", [qps], [("q", pt)], lambda en, pt=pt, qps=qps: en.copy(out=q_sb[:, pt, :], in_=qps.t[:]))
                for pair in range(2):
                    o_ps = ps.alloc()
                    d_ps = ps.alloc()
                    for hh in range(2):
                        h = pair * 2 + hh
                        p0 = hh * 64
                        pTs = []
                        for mc in range(2):
                            sps = ps.alloc()
                            op("pe", [("q", pair), "kT"], [sps], lambda en, mc=mc, sps=sps, p0=p0: en.matmul(
                                sps.t[:], lhsT=kT_sb[p0:p0 + 64, pair, mc * 128:(mc + 1) * 128], rhs=q_sb[p0:p0 + 64, pair, :],
                                start=True, stop=True))
                            pT = rb.alloc()
                            op("act", [sps], [pT], lambda en, sps=sps, pT=pT: en.activation(
                                out=pT.t[:], in_=sps.t[:], func=AF.Exp, scale=0.125))
                            pTs.append(pT)

                        def f(en, h=h, p0=p0, pTs=pTs, o_ps=o_ps, d_ps=d_ps):
                            for mc in range(2):
                                en.matmul(o_ps.t[p0:p0 + 64, :], lhsT=v_sb[:, mc, h * 64:(h + 1) * 64], rhs=pTs[mc].t[:],
                                          start=(mc == 0), stop=(mc == 1))
                            for mc in range(2):
                                i = en.matmul(d_ps.t[p0:p0 + 64, :], lhsT=ones_b[:, 0:64], rhs=pTs[mc].t[:],
                                              start=(mc == 0), stop=(mc == 1))
                            return i
                        op("pe", pTs + ["v", "ones"], [o_ps, d_ps], f)
                    rden = rf.alloc()
                    op("dve", [d_ps], [rden], lambda en, rden=rden, d_ps=d_ps: en.reciprocal(out=rden.t[:, 0:T], in_=d_ps.t[:]))
                    t_ = rf.alloc()
                    op("dve", [o_ps, rden], [t_], lambda en, t_=t_, o_ps=o_ps, rden=rden: en.tensor_tensor(
                        out=t_.t[:, 0:T], in0=o_ps.t[:], in1=rden.t[:, 0:T], op=ALU.mult))
                    sg = silu_gate(26 + pair)
                    op("dve", [t_, sg], [("hcat", 8 + pair)], lambda en, pair=pair, t_=t_, sg=sg: en.tensor_tensor(
                        out=hcat[:, 8 + pair, :], in0=t_.t[:, 0:T], in1=sg.t[:], op=ALU.mult))
                tb = ps.alloc()
                tbb = tb.t[:].bitcast(BF16)

                def f(en, tbb=tbb):
                    for j in range(NT):
                        for pt in range(2):
                            i = en.transpose(tbb[:, pt * T + j * 128: pt * T + (j + 1) * 128], hn_tm[j][:, pt * 128:(pt + 1) * 128], ident_b[:])
                    return i
                op("pe", [("hn", j) for j in range(NT)] + ["identb"], [tb], f)
                sfm = []
                for pt in range(2):
                    s_ = rb.alloc()
                    op("act", [tb, "pcm"], [s_], lambda en, pt=pt, s_=s_, tbb=tbb: en.activation(
                        out=s_.t[:], in_=tbb[:, pt * T:(pt + 1) * T], func=AF.Silu, scale=pc(pt, 38), bias=pc(pt, 39)))
                    sfm.append(s_)
                for po in range(2):
                    pw = ps.alloc()

                    def f(en, po=po, pw=pw, sfm=sfm):
                        for kc in range(2):
                            i = en.matmul(pw.t[:], lhsT=w_pw_sb[:, kc, po * 128:(po + 1) * 128], rhs=sfm[kc].t[:],
                                          start=(kc == 0), stop=(kc == 1))
                        return i
                    op("pe", sfm + ["w_pw"], [pw], f)
                    sg = silu_gate(24 + po)
                    op("dve", [pw, sg], [("hcat", 6 + po)], lambda en, po=po, pw=pw, sg=sg: en.tensor_tensor(
                        out=hcat[:, 6 + po, :], in0=pw.t[:], in1=sg.t[:], op=ALU.mult))
                mp = [ps.alloc(), ps.alloc()]

                def f(en, mp=mp):
                    for j in range(NT):
                        for h in range(4):
                            p0 = (h % 2) * 64
                            i = en.matmul(mp[h // 2].t[p0:p0 + 64, j * 128:(j + 1) * 128], lhsT=vn_tm[j][:, h * 64:(h + 1) * 64],
                                          rhs=wT_sb[:, h * 128:(h + 1) * 128], start=True, stop=True)
                    return i
                op("pe", [("vn", j) for j in range(NT)] + ["wT"], mp, f)
                for pt in range(2):
                    sg = silu_gate(20 + pt)
                    op("dve", [sg, ("gu", pt)], [sg], lambda en, pt=pt, sg=sg: en.tensor_tensor(
                        out=sg.t[:], in0=sg.t[:], in1=gelu_u[:, pt, :], op=ALU.mult))
                    z = rf.alloc()
                    op("dve", [mp[pt], "pcm", "biasB"], [z], lambda en, pt=pt, z=z, mp=mp: en.scalar_tensor_tensor(
                        out=z.t[:, 0:T], in0=mp[pt].t[:], scalar=pc(pt, 3), in1=BiasB[:, pt, :], op0=ALU.mult, op1=ALU.add))
                    op("dve", [z, sg], [("hcat", 2 + pt)], lambda en, pt=pt, z=z, sg=sg: en.tensor_tensor(
                        out=hcat[:, 2 + pt, :], in0=z.t[:, 0:T], in1=sg.t[:], op=ALU.mult))
                cp = [ps.alloc(), ps.alloc()]
                for g in range(4):
                    pt, p0 = g // 2, (g % 2) * 64
                    op("pe", [yp[pt], "poolw"], [cp[pt]], lambda en, pt=pt, p0=p0: en.matmul(
                        cp[pt].t[p0:p0 + 64, :], lhsT=pool_w_sb[p0:p0 + 64, pt, :], rhs=yp[pt].t[p0:p0 + 64, :], start=True, stop=True))
                for pt in range(2):
                    sg = silu_gate(22 + pt)
                    op("dve", [cp[pt], sg, "pcm"], [("hcat", 4 + pt)], lambda en, pt=pt, sg=sg, cp=cp: en.scalar_tensor_tensor(
                        out=hcat[:, 4 + pt, :], in0=cp[pt].t[:], scalar=pc(pt, 5), in1=sg.t[:], op0=ALU.mult, op1=ALU.mult))
                for pt in range(2):
                    acc = accs[pt]
                    bps = proj(2 + pt)
                    op("dve", [bps, acc], [acc], lambda en, acc=acc, bps=bps: en.tensor_tensor(
                        out=acc.t[:, 0:T], in0=bps.t[:], in1=acc.t[:, 0:T], op=ALU.mult))
                    sg = silu_gate(18 + pt)
                    op("dve", [acc, sg], [("hcat", pt)], lambda en, pt=pt, acc=acc, sg=sg: en.tensor_tensor(
                        out=hcat[:, pt, :], in0=acc.t[:, 0:T], in1=sg.t[:], op=ALU.mult))
                if b + 1 < NBLK:
                    emit_xT()
                    if b + 2 < NBLK:
                        load_x(b + 2)
                hck = [("hcat", kc) for kc in range(10)]
                pend = []
                for j in range(NT):
                    rs_ = rcount[0] % 2
                    rcount[0] += 1
                    r = r_buf[rs_]
                    rk = ("r", rs_)
                    r0 = b * T + j * 128
                    dma("sp", [("xd", l, b, j)], [rk], r[:], xsrc[r0:r0 + 128, :], ("rld", rs_))
                    yb = [ps.alloc(), ps.alloc()]

                    def f(en, j=j, yb=yb):
                        for hf in range(2):
                            for kc in range(10):
                                i = en.matmul(yb[hf].t[:], lhsT=hcat[:, kc, j * 128:(j + 1) * 128], rhs=w_out_sb[:, kc, hf * 512:(hf + 1) * 512],
                                              start=(kc == 0), stop=(kc == 9))
                        return i
                    op("pe", hck + ["w_out"], yb, f)

                    def f(en, r=r, yb=yb):
                        for hf in range(2):
                            i = en.scalar_tensor_tensor(out=r[:, hf * 512:(hf + 1) * 512], in0=r[:, hf * 512:(hf + 1) * 512], scalar=ALPHA,
                                                        in1=yb[hf].t[:], op0=ALU.mult, op1=ALU.add)
                        return i
                    op("dve", [rk] + yb, [rk], f)
                    si = ln_a(r, 2, 512, [rk])
                    pend.append((j, r, rk, rs_, r0, si))
                    if len(pend) == 2 or j == NT - 1:
                        for _ in range(1 if j < NT - 1 else len(pend)):
                            (j0, rr, rk0, rs0, r00, si0) = pend.pop(0)
                            ln_b(si0)
                            op("act", [rk0, ("stv1", si0), ("stv2", si0)], [rk0], lambda en, rr=rr, si0=si0: en.activation(
                                out=rr[:], in_=rr[:], func=AF.Identity, scale=stv[si0][:, 1:2], bias=stv[si0][:, 2:3]))
                            op("pool", [rk0, "lng"], [rk0], lambda en, rr=rr: en.tensor_tensor(out=rr[:], in0=rr[:], in1=lng_bc[:], op=ALU.mult))
                            op("pool", [rk0, "lnb"], [rk0], lambda en, rr=rr: en.tensor_tensor(out=rr[:], in0=rr[:], in1=lnb_bc[:], op=ALU.add))
                            dma("sp", [rk0], [("xd", l + 1, b, j0)], xdst[r00:r00 + 128, :], rr[:], ("rst", rs0))
                chk("blk%d" % b)
        except StopBuild:
            pass
        for e in tr.engs:
            if tr.cnt[e] > 0:
                tr._wait("sp", (tr.sem[e], tr.cnt[e], e))
        for k, s in tr.dsem.items():
            tr._wait("sp", (s[0], s[1], "dma:" + str(k)))
    return nc


def _consts():
    ident = np.eye(128, dtype=np.float32)
    mask = np.tril(np.ones((128, 128), dtype=np.float32))
    wins = [2, 4, 8, 16]
    invw = np.zeros((128, 2), np.float32)
    invcnt = np.zeros((128, 32), np.float32)
    for pt in range(2):
        for half in range(2):
            w = wins[pt * 2 + half]
            invw[half * 64:(half + 1) * 64, pt] = 1.0 / w
            for t in range(16):
                invcnt[half * 64:(half + 1) * 64, pt * 16 + t] = 1.0 / min(t + 1, w)
    return {"c_ident": ident, "c_mask": mask, "c_invw": invw, "c_invcnt": invcnt}


_NC_CACHE = {}
FUSED = True


def kernel(**inputs):
    x = np.ascontiguousarray(inputs["x"], dtype=np.float32)
    mem = np.ascontiguousarray(inputs["mem"], dtype=np.float32)
    consts = _consts()
    L = inputs["w_in"].shape[0]
    if FUSED:
        if L not in _NC_CACHE:
            _NC_CACHE[L] = build(L)
        nc = _NC_CACHE[L]
        in_maps = []
        for c in range(N_CORES):
            m = {"x": x[c], "mem": mem[c]}
            for p in PARAMS:
                m[p] = np.ascontiguousarray(inputs[p], dtype=np.float32)
            m.update(consts)
            in_maps.append(m)
        res = run_bass_kernel_spmd(nc, in_maps, core_ids=list(range(N_CORES)))
        return np.stack([np.asarray(r["y"]) for r in res.results], axis=0).astype(np.float32)
    if 1 not in _NC_CACHE:
        _NC_CACHE[1] = build(1)
    nc = _NC_CACHE[1]
    cur = x
    for l in range(L):
        in_maps = []
        for c in range(N_CORES):
            m = {"x": np.ascontiguousarray(cur[c]), "mem": mem[c]}
            for p in PARAMS:
                m[p] = np.ascontiguousarray(inputs[p][l:l + 1], dtype=np.float32)
            m.update(consts)
            in_maps.append(m)
        res = run_bass_kernel_spmd(nc, in_maps, core_ids=list(range(N_CORES)))
        cur = np.stack([np.asarray(r["y"]) for r in res.results], axis=0).astype(np.float32)
    return cur
```

```python
import numpy as np
from contextlib import ExitStack
import concourse.bass as bass
import concourse.mybir as mybir
from concourse.bass_utils import run_bass_kernel_spmd

F32, BF16 = mybir.dt.float32, mybir.dt.bfloat16
AF = mybir.ActivationFunctionType
ALU = mybir.AluOpType

D = 1024; S = 4096; DG = 256; DIN = 3584; DMIX = 1280; MEM = 256
T = 512; NT = 4; NBLK = S // T
ALPHA = float((2.0 * 2) ** 0.25)
EPS = 1e-5
N_CORES = 8
PARAMS = ["w_in", "conv_a_w", "sg_ln_g", "sg_ln_b", "sg_w", "sg_b", "pool_w", "pool_scale",
          "cc_dw_w", "cc_dw_b", "cc_ln_g", "cc_ln_b", "cc_pw_w", "w_kv", "w_out", "ln_g", "ln_b"]
PSHAPES = {"w_in": [D, DIN], "conv_a_w": [3, DG], "sg_ln_g": [DG], "sg_ln_b": [DG], "sg_w": [4, 128, 128],
           "sg_b": [4, 128], "pool_w": [4, 64, 64], "pool_scale": [DG], "cc_dw_w": [31, DG], "cc_dw_b": [DG],
           "cc_ln_g": [DG], "cc_ln_b": [DG], "cc_pw_w": [DG, DG], "w_kv": [D, 2 * DG], "w_out": [DMIX, D],
           "ln_g": [D], "ln_b": [D]}


class H:
    __slots__ = ("t", "key", "ring", "slot", "gen")

    def __init__(self, t, key, ring=None, slot=0, gen=0):
        self.t, self.key, self.ring, self.slot, self.gen = t, key, ring, slot, gen


class Ring:
    def __init__(self, name, tensors):
        self.name, self.tensors, self.i = name, tensors, 0
        self.gen = [0] * len(tensors)

    def alloc(self):
        s = self.i % len(self.tensors)
        self.i += 1
        self.gen[s] += 1
        return H(self.tensors[s], (self.name, s), self, s, self.gen[s])


class Tr:
    def __init__(self, nc, es):
        self.nc, self.es = nc, es
        self.engs = {"pe": nc.tensor, "act": nc.scalar, "dve": nc.vector, "pool": nc.gpsimd, "sp": nc.sync}
        self.sem = {e: es.enter_context(nc.semaphore("s_" + e)) for e in self.engs}
        self.cnt = {e: 0 for e in self.engs}
        self.waited = {e: {} for e in self.engs}
        self.lastw = {}
        self.readers = {}
        self.dsem = {}

    def _key(self, b):
        if isinstance(b, H):
            assert b.ring is None or b.ring.gen[b.slot] == b.gen, f"ring buffer {b.key} reused while live"
            return b.key
        return b

    def _wait(self, e, tok):
        sem, val, sid = tok
        if isinstance(sid, str) and sid.startswith("dma:"):
            for s_ in self.dsem.values():
                if s_[0] is sem:
                    val = max(val, s_[1])
        if self.waited[e].get(sid, 0) >= val:
            return
        self.engs[e].wait_ge(sem, val)
        self.waited[e][sid] = val

    def op(self, e, reads, writes, fn, dma=None):
        rk = [self._key(b) for b in reads]
        wk = [self._key(b) for b in writes]
        deps = []
        for k in rk:
            if k in self.lastw:
                deps.append(self.lastw[k])
        for k in wk:
            if k in self.lastw:
                deps.append(self.lastw[k])
            deps.extend(self.readers.get(k, {}).values())
        for tok in deps:
            if e == "pe" and tok[2] == "pe":
                continue
            self._wait(e, tok)
        inst = fn(self.engs[e])
        if dma is None:
            self.cnt[e] += 1
            inst.then_inc(self.sem[e], 1)
            tok = (self.sem[e], self.cnt[e], e)
        else:
            if dma not in self.dsem:
                self.dsem[dma] = [self.es.enter_context(self.nc.semaphore("d_%d" % len(self.dsem))), 0]
            s = self.dsem[dma]
            s[1] += 16
            inst.then_inc(s[0], 16)
            tok = (s[0], s[1], "dma:" + str(dma))
        for k in rk:
            d = self.readers.setdefault(k, {})
            old = d.get(tok[2])
            if old is None or old[1] < tok[1]:
                d[tok[2]] = tok
        for k in wk:
            self.lastw[k] = tok
            self.readers[k] = {}
        return tok


class StopBuild(Exception):
    pass


STOP = None


def chk(name):
    if STOP == name:
        raise StopBuild()


def build(nl):
    nc = bass.Bass("TRN2", target_bir_lowering=False)
    dr = {}
    dr["x"] = nc.dram_tensor("x", [S, D], F32, kind="ExternalInput").ap()
    dr["mem"] = nc.dram_tensor("mem", [MEM, D], F32, kind="ExternalInput").ap()
    for p in PARAMS:
        dr[p] = nc.dram_tensor(p, [nl] + PSHAPES[p], F32, kind="ExternalInput").ap()
    dr["c_ident"] = nc.dram_tensor("c_ident", [128, 128], F32, kind="ExternalInput").ap()
    dr["c_mask"] = nc.dram_tensor("c_mask", [128, 128], F32, kind="ExternalInput").ap()
    dr["c_invw"] = nc.dram_tensor("c_invw", [128, 2], F32, kind="ExternalInput").ap()
    dr["c_invcnt"] = nc.dram_tensor("c_invcnt", [128, 32], F32, kind="ExternalInput").ap()
    y_out = nc.dram_tensor("y", [S, D], F32, kind="ExternalOutput").ap()
    xscr = [nc.dram_tensor("xscr%d" % i, [S, D], F32, kind="Internal").ap() for i in range(nl - 1)]

    es = ExitStack()
    with es:
        def sb(name, shape, dt):
            return es.enter_context(nc.sbuf_tensor(name, shape, dt))

        tr = Tr(nc, es)
        op = tr.op
        ident_f = sb("ident_f", [128, 128], F32); ident_b = sb("ident_b", [128, 128], BF16)
        mask_f = sb("mask_f", [128, 128], F32)
        invw = sb("invw", [128, 2], F32); invcnt = sb("invcnt", [128, 32], F32)
        ones_b = sb("ones_b", [128, 128], BF16); neghalf = sb("neghalf", [128, 4], F32)
        ones_row = sb("ones_row", [1, 128], BF16)
        w_in_sb = sb("w_in_sb", [128, 8, DIN], BF16)
        w_out_sb = sb("w_out_sb", [128, 10, D], BF16)
        w_pw_sb = sb("w_pw_sb", [128, 2, DG], BF16)
        pool_w_sb = sb("pool_w_sb", [128, 2, 64], BF16)
        PR = sb("PR", [40, DG], F32)
        pcm = sb("pcm", [128, 80], F32)
        bdw_row = sb("bdw_row", [1, DG], BF16)
        lng_bc = sb("lng_bc", [128, D], F32); lnb_bc = sb("lnb_bc", [128, D], F32)
        wT_sb = sb("wT_sb", [128, 512], BF16)
        BSt = sb("BSt", [128, 2, 128], F32)
        BiasB = sb("BiasB", [128, 2, 512], F32)
        diag_sb = sb("diag_sb", [128, 62 * 128], BF16)
        memT = sb("memT", [128, 8, MEM], BF16)
        kT_sb = sb("kT_sb", [128, 2, MEM], BF16)
        v_sb = sb("v_sb", [128, 2, DG], BF16)
        x_bf = sb("x_bf", [128, NT, D], BF16)
        xT = sb("xT", [128, 8, T], BF16)
        hcat = sb("hcat", [128, 10, T], BF16)
        m_ext = [sb("m_ext%d" % i, [128, 2, T + 2], F32) for i in range(2)]
        xc_ext = [sb("xc_ext%d" % i, [128, 2, T + 15], F32) for i in range(2)]
        h_ext = [sb("h_ext%d" % i, [128, 2, T + 30], BF16) for i in range(2)]
        r_buf = [sb("r_buf%d" % i, [128, D], F32) for i in range(2)]
        vn_tm = [sb("vn_tm%d" % i, [128, DG], BF16) for i in range(NT)]
        hn_tm = [sb("hn_tm%d" % i, [128, DG], BF16) for i in range(NT)]
        gelu_u = sb("gelu_u", [128, 2, T], BF16)
        q_sb = sb("q_sb", [128, 2, T], BF16)
        ctmp = sb("ctmp", [128, 16], F32)
        acc_buf = [sb("acc_buf%d" % i, [128, T], F32) for i in range(2)]
        yp_buf = [sb("yp_buf%d" % i, [128, T], BF16) for i in range(2)]
        NST = 4
        NSTT = 8
        st6 = [sb("st6_%d" % i, [128, 12], F32) for i in range(NSTT)]
        stmv = [sb("stmv_%d" % i, [128, 2], F32) for i in range(NSTT)]
        stv = [sb("stv_%d" % i, [128, 4], F32) for i in range(NSTT)]
        stc = [0, 0]
        rf = Ring("rf", [sb("rf%d" % i, [128, 544], F32) for i in range(7)])
        rb = Ring("rb", [sb("rb%d" % i, [128, T], BF16) for i in range(6)])
        ps = Ring("ps", [es.enter_context(nc.psum_tensor("ps%d" % i, [128, 512], F32)) for i in range(8)])

        def pc(pt, r):
            return pcm[:, pt * 40 + r: pt * 40 + r + 1]

        def dma(e, reads, writes, out, in_, key, **kw):
            return op(e, reads, writes, lambda en: en.dma_start(out=out, in_=in_, **kw), dma=key)

        dma("sp", [], ["ident_f"], ident_f[:], dr["c_ident"], "consts0")
        dma("sp", [], ["mask"], mask_f[:], dr["c_mask"], "consts1")
        dma("sp", [], ["invw"], invw[:], dr["c_invw"], "consts2")
        dma("sp", [], ["invcnt"], invcnt[:], dr["c_invcnt"], "consts3")
        op("dve", ["ident_f"], ["identb"], lambda en: en.tensor_copy(out=ident_b[:], in_=ident_f[:]))
        op("dve", [], ["ones"], lambda en: en.memset(ones_b[:], 1.0))
        op("dve", [], ["ones_row"], lambda en: en.memset(ones_row[:], 1.0))
        op("pool", [], ["neghalf"], lambda en: en.memset(neghalf[:], -0.5))
        for mc in range(2):
            dma("pool", [], [("x_bf", mc)], x_bf[:, mc, :], dr["mem"][mc * 128:(mc + 1) * 128, :], ("xbf", mc))
        for kc in range(8):
            tb = ps.alloc()
            tbb = tb.t[:].bitcast(BF16)

            def f(en, kc=kc, tbb=tbb):
                for mc in range(2):
                    i = en.transpose(tbb[:, mc * 128:(mc + 1) * 128], x_bf[:, mc, kc * 128:(kc + 1) * 128], ident_b[:])
                return i
            op("pe", [("x_bf", 0), ("x_bf", 1), "identb"], [tb], f)
            op("dve", [tb], ["memT"], lambda en, kc=kc, tbb=tbb: en.tensor_copy(out=memT[:, kc, :], in_=tbb[:, 0:MEM]))

        def ln_a(src_ap, nchunk, csz, src_keys, grp=0):
            si = grp * NST + stc[grp] % NST
            stc[grp] += 1
            mv, sv = stmv[si], stv[si]

            def f(en):
                for c in range(nchunk):
                    i = en.bn_stats(out=st6[si][:, c * 6:(c + 1) * 6], in_=src_ap[:, c * csz:(c + 1) * csz])
                return i
            op("dve", src_keys, [("st6", si)], f)
            op("dve", [("st6", si)], [("stmv", si)], lambda en: en.bn_aggr(out=mv[:], in_=st6[si][:, 0:6 * nchunk]))
            op("dve", [("stmv", si)], [("stv0", si)], lambda en: en.tensor_scalar(
                out=sv[:, 0:1], in0=mv[:, 1:2], scalar1=EPS, scalar2=None, op0=ALU.add))
            op("pool", [("stv0", si), "neghalf"], [("stv1", si)], lambda en: en.tensor_tensor(
                out=sv[:, 1:2], in0=sv[:, 0:1], in1=neghalf[:, 0:1], op=ALU.pow))
            return si

        def ln_b(si):
            mv, sv = stmv[si], stv[si]
            op("dve", [("stmv", si), ("stv1", si)], [("stv2", si)], lambda en: en.scalar_tensor_tensor(
                out=sv[:, 2:3], in0=mv[:, 0:1], scalar=-1.0, in1=sv[:, 1:2], op0=ALU.mult, op1=ALU.mult))

        xT_keys = [("xT", kc) for kc in range(8)]

        def proj(col):
            bank = ps.alloc()

            def f(en):
                for kc in range(8):
                    i = en.matmul(bank.t[:], lhsT=w_in_sb[:, kc, col * 128:(col + 1) * 128], rhs=xT[:, kc, :],
                                  start=(kc == 0), stop=(kc == 7))
                return i
            op("pe", xT_keys + [("w_in", col // 2)], [bank], f)
            return bank

        def silu_gate(col):
            g = proj(col)
            sg = rb.alloc()
            op("act", [g], [sg], lambda en: en.activation(out=sg.t[:], in_=g.t[:], func=AF.Silu))
            return sg

        try:
          chk("setup0")
          for l in range(nl):
            xsrc = dr["x"] if l == 0 else xscr[l - 1]
            xdst = y_out if l == nl - 1 else xscr[l]

            def load_PR(l):
                dma("sp", [], ["PR"], PR[0:3, :], dr["conv_a_w"][l], "PR")
                dma("sp", [], ["PR"], PR[3:4, :], dr["sg_ln_g"][l:l + 1, :], "PR")
                dma("sp", [], ["PR"], PR[4:5, :], dr["sg_ln_b"][l:l + 1, :], "PR")
                dma("sp", [], ["PR"], PR[5:6, :], dr["pool_scale"][l:l + 1, :], "PR")
                dma("sp", [], ["PR"], PR[6:37, :], dr["cc_dw_w"][l], "PR")
                dma("sp", [], ["PR"], PR[37:38, :], dr["cc_dw_b"][l:l + 1, :], "PR")
                dma("sp", [], ["PR"], PR[38:39, :], dr["cc_ln_g"][l:l + 1, :], "PR")
                dma("sp", [], ["PR"], PR[39:40, :], dr["cc_ln_b"][l:l + 1, :], "PR")
                for h in range(4):
                    p0 = (h % 2) * 64
                    dma("sp", [], ["BSt"], BSt[p0:p0 + 64, h // 2, :], dr["sg_b"][l][h].partition_broadcast(64), "BSt")

            if l == 0:
                load_PR(0)
            def load_x(b):
                for j in range(NT):
                    r0 = b * T + j * 128
                    dma("pool", [("xd", l, b, j)], [("x_bf", j)], x_bf[:, j, :], xsrc[r0:r0 + 128, :], ("xbf", j))

            def emit_xT():
                for kc in range(8):
                    tb = ps.alloc()
                    tbb = tb.t[:].bitcast(BF16)

                    def f(en, kc=kc, tbb=tbb):
                        for j in range(NT):
                            i = en.transpose(tbb[:, j * 128:(j + 1) * 128], x_bf[:, j, kc * 128:(kc + 1) * 128], ident_b[:])
                        return i
                    op("pe", [("x_bf", j) for j in range(NT)] + ["identb"], [tb], f)
                    if kc % 2 == 0:
                        op("act", [tb], [("xT", kc)], lambda en, kc=kc, tbb=tbb: en.copy(out=xT[:, kc, :], in_=tbb[:, 0:T]))
                    else:
                        op("dve", [tb], [("xT", kc)], lambda en, kc=kc, tbb=tbb: en.tensor_copy(out=xT[:, kc, :], in_=tbb[:, 0:T]))

            load_x(0)
            emit_xT()
            load_x(1)
            for kc in range(2):
                dma("pool", [], ["w_pw"], w_pw_sb[:, kc, :], dr["cc_pw_w"][l][kc * 128:(kc + 1) * 128, :], "w_pw")
            for g in range(4):
                p0 = (g % 2) * 64
                dma("pool", [], ["poolw"], pool_w_sb[p0:p0 + 64, g // 2, :], dr["pool_w"][l][g], "poolw")
            dma("pool", [], ["bdw"], bdw_row[0:1, :], dr["cc_dw_b"][l:l + 1, :], "bdw")
            w_in_v = dr["w_in"][l].rearrange("(kc p) n -> p kc n", p=128)

            def load_w_in(g):
                c0 = g * 256
                dma("pool", [], [("w_in", g)], w_in_sb[:, :, c0:c0 + 256], w_in_v[:, :, c0:c0 + 256], ("w_in", g))

            for g in (6, 7, 4, 5, 0, 2, 3, 8):
                load_w_in(g)
            for kc in range(8):
                dma("pool", [], [("hcat", kc)], hcat[:, kc, :], dr["w_kv"][l][kc * 128:(kc + 1) * 128, :], "w_kv")
            for g in (13, 12, 10, 11, 1, 9):
                load_w_in(g)
            for kc in range(10):
                dma("pool", [], ["w_out"], w_out_sb[:, kc, :], dr["w_out"][l][kc * 128:(kc + 1) * 128, :], "w_out")
            dma("sp", [], ["lng"], lng_bc[:], dr["ln_g"][l].partition_broadcast(128), "lng")
            dma("sp", [], ["lnb"], lnb_bc[:], dr["ln_b"][l].partition_broadcast(128), "lnb")
            stg = rf.alloc()
            for h in range(4):
                dma("sp", [], [stg], stg.t[:, h * 128:(h + 1) * 128], dr["sg_w"][l][h], "sgw")

            tp = ps.alloc()

            def f(en, tp=tp):
                for pt in range(2):
                    i = en.transpose(tp.t[:, pt * 40:(pt + 1) * 40], PR[0:40, pt * 128:(pt + 1) * 128], ident_f[0:40, 0:40])
                return i
            op("pe", ["PR", "ident_f"], [tp], f)
            op("dve", [tp], ["pcm"], lambda en, tp=tp: en.tensor_copy(out=pcm[:, 0:80], in_=tp.t[:, 0:80]))

            def f(en):
                for pt in range(2):
                    for k in range(31):
                        j = pt * 31 + k
                        i = en.tensor_scalar(out=diag_sb[:, j * 128:(j + 1) * 128], in0=ident_b[:],
                                             scalar1=pc(pt, 6 + k), scalar2=None, op0=ALU.mult)
                return i
            op("dve", ["pcm", "identb"], ["diag"], f)

            def f(en, stg=stg):
                for h in range(4):
                    i = en.tensor_tensor(out=stg.t[:, h * 128:(h + 1) * 128], in0=stg.t[:, h * 128:(h + 1) * 128],
                                         in1=mask_f[:], op=ALU.mult)
                return i
            op("dve", [stg, "mask"], [stg], f)
            tp2 = ps.alloc()

            def f(en, stg=stg, tp2=tp2):
                for h in range(4):
                    i = en.transpose(tp2.t[:, h * 128:(h + 1) * 128], stg.t[:, h * 128:(h + 1) * 128], ident_f[:])
                return i
            op("pe", [stg, "ident_f"], [tp2], f)
            op("dve", [tp2], ["wT"], lambda en, tp2=tp2: en.tensor_copy(out=wT_sb[:], in_=tp2.t[:]))
            rs = [ps.alloc(), ps.alloc()]

            def f(en, rs=rs):
                for h in range(4):
                    p0 = (h % 2) * 64
                    i = en.matmul(rs[h // 2].t[p0:p0 + 64, 0:128], lhsT=ones_b[:, 0:64], rhs=wT_sb[:, h * 128:(h + 1) * 128],
                                  start=True, stop=True)
                return i
            op("pe", ["wT", "ones"], rs, f)
            for pt in range(2):
                def f(en, pt=pt, rs=rs):
                    for rep in range(4):
                        i = en.scalar_tensor_tensor(out=BiasB[:, pt, rep * 128:(rep + 1) * 128], in0=rs[pt].t[:, 0:128],
                                                    scalar=pc(pt, 4), in1=BSt[:, pt, :], op0=ALU.mult, op1=ALU.add)
                    return i
                op("dve", [rs[pt], "pcm", "BSt"], ["biasB"], f)
            def emit_kv():
                hk = [("hcat", kc) for kc in range(8)]
                for pt in range(2):
                    kp = ps.alloc()

                    def f(en, pt=pt, kp=kp):
                        for kc in range(8):
                            i = en.matmul(kp.t[:, 0:MEM], lhsT=hcat[:, kc, pt * 128:(pt + 1) * 128], rhs=memT[:, kc, :],
                                          start=(kc == 0), stop=(kc == 7))
                        return i
                    op("pe", hk + ["memT"], [kp], f)
                    op("dve", [kp], ["kT"], lambda en, pt=pt, kp=kp: en.tensor_copy(out=kT_sb[:, pt, :], in_=kp.t[:, 0:MEM]))
                for mc in range(2):
                    vp = ps.alloc()

                    def f(en, mc=mc, vp=vp):
                        for kc in range(8):
                            i = en.matmul(vp.t[:, 0:DG], lhsT=memT[:, kc, mc * 128:(mc + 1) * 128], rhs=hcat[:, kc, DG:2 * DG],
                                          start=(kc == 0), stop=(kc == 7))
                        return i
                    op("pe", hk + ["memT"], [vp], f)
                    op("dve", [vp], ["v"], lambda en, mc=mc, vp=vp: en.tensor_copy(out=v_sb[:, mc, :], in_=vp.t[:, 0:DG]))


            chk("setupL")

            pend_o = []

            def outproj(bb, js, last):
                hck = [("hcat", kc) for kc in range(10)]
                for j in js:
                    rs_ = rcount[0] % 2
                    rcount[0] += 1
                    r = r_buf[rs_]
                    rk = ("r", rs_)
                    r0 = bb * T + j * 128
                    dma("sp", [("xd", l, bb, j)], [rk], r[:], xsrc[r0:r0 + 128, :], ("rld", rs_))
                    yb = [ps.alloc(), ps.alloc()]

                    def f(en, j=j, yb=yb):
                        for hf in range(2):
                            for kc in range(10):
                                i = en.matmul(yb[hf].t[:], lhsT=hcat[:, kc, j * 128:(j + 1) * 128], rhs=w_out_sb[:, kc, hf * 512:(hf + 1) * 512],
                                              start=(kc == 0), stop=(kc == 9))
                        return i
                    op("pe", hck + ["w_out"], yb, f)

                    def f(en, r=r, yb=yb):
                        for hf in range(2):
                            i = en.scalar_tensor_tensor(out=r[:, hf * 512:(hf + 1) * 512], in0=r[:, hf * 512:(hf + 1) * 512], scalar=ALPHA,
                                                        in1=yb[hf].t[:], op0=ALU.mult, op1=ALU.add)
                        return i
                    op("dve", [rk] + yb, [rk], f)
                    si = ln_a(r, 2, 512, [rk], grp=1)
                    pend_o.append((j, r, rk, rs_, r0, si))
                    if len(pend_o) == 2 or (last and j == js[-1]):
                        for _ in range(len(pend_o) if (last and j == js[-1]) else 1):
                            (j0, rr, rk0, rs0, r00, si0) = pend_o.pop(0)
                            ln_b(si0)
                            op("act", [rk0, ("stv1", si0), ("stv2", si0)], [rk0], lambda en, rr=rr, si0=si0: en.activation(
                                out=rr[:], in_=rr[:], func=AF.Identity, scale=stv[si0][:, 1:2], bias=stv[si0][:, 2:3]))
                            op("pool", [rk0, "lng"], [rk0], lambda en, rr=rr: en.tensor_tensor(out=rr[:], in0=rr[:], in1=lng_bc[:], op=ALU.mult))
                            op("pool", [rk0, "lnb"], [rk0], lambda en, rr=rr: en.tensor_tensor(out=rr[:], in0=rr[:], in1=lnb_bc[:], op=ALU.add))
                            dma("sp", [rk0], [("xd", l + 1, bb, j0)], xdst[r00:r00 + 128, :], rr[:], ("rst", rs0))

            rcount = [0]
            for b in range(NBLK):
                slot, prev = b % 2, 1 - (b % 2)
                for (ext, nm, hl) in ((h_ext, "hext", 30), (xc_ext, "xc", 15), (m_ext, "mext", 2)):
                    wk = [(nm, slot, "halo")]
                    if b == 0:
                        op("pool", [], wk, lambda en, ext=ext, hl=hl: en.memset(ext[slot][:, :, 0:hl], 0.0))
                    else:
                        op("pool", [(nm, prev, 0), (nm, prev, 1)], wk,
                           lambda en, ext=ext, hl=hl: en.tensor_copy(out=ext[slot][:, :, 0:hl], in_=ext[prev][:, :, T:T + hl]))
                da = [proj(12), proj(13)]
                dg = [proj(14), proj(15)]
                for pt in range(2):
                    sig = rf.alloc()
                    op("act", [dg[pt]], [sig], lambda en, pt=pt, sig=sig: en.activation(
                        out=sig.t[:, 0:T], in_=dg[pt].t[:], func=AF.Sigmoid))
                    op("dve", [da[pt], sig], [("hext", slot, pt)], lambda en, pt=pt, sig=sig: en.tensor_tensor(
                        out=h_ext[slot][:, pt, 30:30 + T], in0=da[pt].t[:], in1=sig.t[:, 0:T], op=ALU.mult))
                if b > 0:
                    outproj(b - 1, [0, 1], False)
                pend = []
                for j in range(NT):
                    vps = ps.alloc()

                    def f(en, j=j, vps=vps):
                        for kc in range(8):
                            i = en.matmul(vps.t[:, 0:DG], lhsT=xT[:, kc, j * 128:(j + 1) * 128], rhs=w_in_sb[:, kc, 1024:1280],
                                          start=(kc == 0), stop=(kc == 7))
                        return i
                    op("pe", xT_keys + [("w_in", 4)], [vps], f)
                    gv = rf.alloc()
                    op("act", [vps], [gv], lambda en, vps=vps, gv=gv: en.activation(
                        out=gv.t[:, 0:DG], in_=vps.t[:, 0:DG], func=AF.Gelu_apprx_tanh))
                    si = ln_a(gv.t, 1, DG, [gv])
                    pend.append((j, gv, si))
                    if len(pend) == 2 or j == NT - 1:
                        for _ in range(1 if j < NT - 1 else len(pend)):
                            (j0, gv0, si0) = pend.pop(0)
                            ln_b(si0)
                            op("act", [gv0, ("stv1", si0), ("stv2", si0)], [("vn", j0)], lambda en, j0=j0, gv0=gv0, si0=si0: en.activation(
                                out=vn_tm[j0][:], in_=gv0.t[:, 0:DG], func=AF.Identity, scale=stv[si0][:, 1:2], bias=stv[si0][:, 2:3]))
                if b > 0:
                    outproj(b - 1, [2, 3], True)
                X = xc_ext[slot]
                W = T + 15
                for pt in range(2):
                    xps = proj(10 + pt)
                    op("act", [xps], [("xc", slot, pt)], lambda en, pt=pt, xps=xps: en.copy(out=X[:, pt, 15:W], in_=xps.t[:]))
                xck = [("xc", slot, "halo"), ("xc", slot, 0), ("xc", slot, 1)]
                S2 = [rf.alloc(), rf.alloc()]
                S4 = [rf.alloc(), rf.alloc()]
                for pt in range(2):
                    op("dve", xck, [S2[pt]], lambda en, pt=pt: en.tensor_tensor(
                        out=S2[pt].t[:, 1:W], in0=X[:, pt, 1:W], in1=X[:, pt, 0:W - 1], op=ALU.add))
                op("dve", [S2[0]], [S4[0]], lambda en: en.tensor_tensor(
                    out=S4[0].t[64:128, 3:W], in0=S2[0].t[64:128, 3:W], in1=S2[0].t[64:128, 1:W - 2], op=ALU.add))
                op("dve", [S2[1]], [S4[1]], lambda en: en.tensor_tensor(
                    out=S4[1].t[:, 3:W], in0=S2[1].t[:, 3:W], in1=S2[1].t[:, 1:W - 2], op=ALU.add))
                op("dve", [S4[1]], [S2[1]], lambda en: en.tensor_tensor(
                    out=S2[1].t[:, 7:W], in0=S4[1].t[:, 7:W], in1=S4[1].t[:, 3:W - 4], op=ALU.add))
                op("dve", [S2[1]], [S4[1]], lambda en: en.tensor_tensor(
                    out=S4[1].t[64:128, 15:W], in0=S2[1].t[64:128, 15:W], in1=S2[1].t[64:128, 7:W - 8], op=ALU.add))
                yp = [H(yp_buf[0], ("yp", 0)), H(yp_buf[1], ("yp", 1))]
                srcs = [(0, 0, S2[0]), (0, 64, S4[0]), (1, 0, S2[1]), (1, 64, S4[1])]
                for (pt, p0, Sx) in srcs:
                    op("dve", [Sx, "invw"] + xck, [yp[pt]], lambda en, pt=pt, p0=p0, Sx=Sx: en.scalar_tensor_tensor(
                        out=yp[pt].t[p0:p0 + 64, :], in0=Sx.t[p0:p0 + 64, 15:W], scalar=invw[p0:p0 + 64, pt:pt + 1],
                        in1=X[p0:p0 + 64, pt, 15:W], op0=ALU.mult, op1=ALU.subtract))
                    if b == 0:
                        op("dve", [Sx, "invcnt"], ["ctmp"], lambda en, pt=pt, p0=p0, Sx=Sx: en.tensor_tensor(
                            out=ctmp[p0:p0 + 64, :], in0=Sx.t[p0:p0 + 64, 15:31], in1=invcnt[p0:p0 + 64, pt * 16:(pt + 1) * 16], op=ALU.mult))
                        op("dve", ["ctmp"] + xck, [yp[pt]], lambda en, pt=pt, p0=p0: en.tensor_tensor(
                            out=yp[pt].t[p0:p0 + 64, 0:16], in0=ctmp[p0:p0 + 64, :], in1=X[p0:p0 + 64, pt, 15:31], op=ALU.subtract))
                M = m_ext[slot]
                xa = [proj(0), proj(1)]
                ca = [proj(4), proj(5)]
                for pt in range(2):
                    xs = rf.alloc()
                    op("act", [xa[pt]], [xs], lambda en, pt=pt, xs=xs: en.copy(out=xs.t[:, 0:T], in_=xa[pt].t[:]))
                    op("dve", [ca[pt], xs], [("mext", slot, pt)], lambda en, pt=pt, xs=xs: en.tensor_tensor(
                        out=M[:, pt, 2:2 + T], in0=ca[pt].t[:], in1=xs.t[:, 0:T], op=ALU.mult))
                mk = [("mext", slot, "halo"), ("mext", slot, 0), ("mext", slot, 1)]
                accs = []
                for pt in range(2):
                    acc = H(acc_buf[pt], ("acc", pt))
                    accs.append(acc)
                    op("dve", mk + ["pcm"], [acc], lambda en, pt=pt, acc=acc: en.tensor_scalar(
                        out=acc.t[:, 0:T], in0=M[:, pt, 2:2 + T], scalar1=pc(pt, 2), scalar2=None, op0=ALU.mult))
                    op("dve", mk + ["pcm", acc], [acc], lambda en, pt=pt, acc=acc: en.scalar_tensor_tensor(
                        out=acc.t[:, 0:T], in0=M[:, pt, 1:1 + T], scalar=pc(pt, 1), in1=acc.t[:, 0:T], op0=ALU.mult, op1=ALU.add))
                    op("dve", mk + ["pcm", acc], [acc], lambda en, pt=pt, acc=acc: en.scalar_tensor_tensor(
                        out=acc.t[:, 0:T], in0=M[:, pt, 0:T], scalar=pc(pt, 0), in1=acc.t[:, 0:T], op0=ALU.mult, op1=ALU.add))
                hkeys = [("hext", slot, "halo"), ("hext", slot, 0), ("hext", slot, 1)]
                pend = []
                for j in range(NT):
                    cps = ps.alloc()

                    def f(en, j=j, cps=cps):
                        for pt in range(2):
                            for k in range(31):
                                en.matmul(cps.t[:, pt * 128:(pt + 1) * 128], lhsT=h_ext[slot][:, pt, j * 128 + k: j * 128 + k + 128],
                                          rhs=diag_sb[:, (pt * 31 + k) * 128:(pt * 31 + k + 1) * 128], start=(k == 0), stop=False)
                            i = en.matmul(cps.t[:, pt * 128:(pt + 1) * 128], lhsT=ones_row[0:1, 0:128],
                                          rhs=bdw_row[0:1, pt * 128:(pt + 1) * 128], start=False, stop=True)
                        return i
                    op("pe", hkeys + ["diag", "bdw", "ones_row"], [cps], f)
                    si = ln_a(cps.t, 1, DG, [cps])
                    pend.append((j, cps, si))
                    if len(pend) == 2 or j == NT - 1:
                        for _ in range(1 if j < NT - 1 else len(pend)):
                            (j0, cps0, si0) = pend.pop(0)
                            ln_b(si0)
                            op("act", [cps0, ("stv1", si0), ("stv2", si0)], [("hn", j0)], lambda en, j0=j0, cps0=cps0, si0=si0: en.activation(
                                out=hn_tm[j0][:], in_=cps0.t[:, 0:DG], func=AF.Identity, scale=stv[si0][:, 1:2], bias=stv[si0][:, 2:3]))
                for pt in range(2):
                    ups = proj(6 + pt)
                    op("act", [ups], [("gu", pt)], lambda en, pt=pt, ups=ups: en.activation(
                        out=gelu_u[:, pt, :], in_=ups.t[:], func=AF.Gelu_apprx_tanh))
                for pt in range(2):
                    qps = proj(16 + pt)
                    op("act", [qps], [("q", pt)], lambda en, pt=pt, qps=qps: en.copy(out=q_sb[:, pt, :], in_=qps.t[:]))
                if b == 0:
                    emit_kv()
                for pair in range(2):
                    o_ps = ps.alloc()
                    d_ps = ps.alloc()
                    for hh in range(2):
                        h = pair * 2 + hh
                        p0 = hh * 64
                        pTs = []
                        for mc in range(2):
                            sps = ps.alloc()
                            op("pe", [("q", pair), "kT"], [sps], lambda en, mc=mc, sps=sps, p0=p0: en.matmul(
                                sps.t[:], lhsT=kT_sb[p0:p0 + 64, pair, mc * 128:(mc + 1) * 128], rhs=q_sb[p0:p0 + 64, pair, :],
                                start=True, stop=True))
                            pT = rb.alloc()
                            op("act", [sps], [pT], lambda en, sps=sps, pT=pT: en.activation(
                                out=pT.t[:], in_=sps.t[:], func=AF.Exp, scale=0.125))
                            pTs.append(pT)

                        def f(en, h=h, p0=p0, pTs=pTs, o_ps=o_ps, d_ps=d_ps):
                            for mc in range(2):
                                en.matmul(o_ps.t[p0:p0 + 64, :], lhsT=v_sb[:, mc, h * 64:(h + 1) * 64], rhs=pTs[mc].t[:],
                                          start=(mc == 0), stop=(mc == 1))
                            for mc in range(2):
                                i = en.matmul(d_ps.t[p0:p0 + 64, :], lhsT=ones_b[:, 0:64], rhs=pTs[mc].t[:],
                                              start=(mc == 0), stop=(mc == 1))
                            return i
                        op("pe", pTs + ["v", "ones"], [o_ps, d_ps], f)
                    rden = rf.alloc()
                    op("dve", [d_ps], [rden], lambda en, rden=rden, d_ps=d_ps: en.reciprocal(out=rden.t[:, 0:T], in_=d_ps.t[:]))
                    t_ = rf.alloc()
                    op("dve", [o_ps, rden], [t_], lambda en, t_=t_, o_ps=o_ps, rden=rden: en.tensor_tensor(
                        out=t_.t[:, 0:T], in0=o_ps.t[:], in1=rden.t[:, 0:T], op=ALU.mult))
                    sg = silu_gate(26 + pair)
                    op("dve", [t_, sg], [("hcat", 8 + pair)], lambda en, pair=pair, t_=t_, sg=sg: en.tensor_tensor(
                        out=hcat[:, 8 + pair, :], in0=t_.t[:, 0:T], in1=sg.t[:], op=ALU.mult))
                tb = ps.alloc()
                tbb = tb.t[:].bitcast(BF16)

                def f(en, tbb=tbb):
                    for j in range(NT):
                        for pt in range(2):
                            i = en.transpose(tbb[:, pt * T + j * 128: pt * T + (j + 1) * 128], hn_tm[j][:, pt * 128:(pt + 1) * 128], ident_b[:])
                    return i
                op("pe", [("hn", j) for j in range(NT)] + ["identb"], [tb], f)
                sfm = []
                for pt in range(2):
                    s_ = rb.alloc()
                    op("act", [tb, "pcm"], [s_], lambda en, pt=pt, s_=s_, tbb=tbb: en.activation(
                        out=s_.t[:], in_=tbb[:, pt * T:(pt + 1) * T], func=AF.Silu, scale=pc(pt, 38), bias=pc(pt, 39)))
                    sfm.append(s_)
                for po in range(2):
                    pw = ps.alloc()

                    def f(en, po=po, pw=pw, sfm=sfm):
                        for kc in range(2):
                            i = en.matmul(pw.t[:], lhsT=w_pw_sb[:, kc, po * 128:(po + 1) * 128], rhs=sfm[kc].t[:],
                                          start=(kc == 0), stop=(kc == 1))
                        return i
                    op("pe", sfm + ["w_pw"], [pw], f)
                    sg = silu_gate(24 + po)
                    op("dve", [pw, sg], [("hcat", 6 + po)], lambda en, po=po, pw=pw, sg=sg: en.tensor_tensor(
                        out=hcat[:, 6 + po, :], in0=pw.t[:], in1=sg.t[:], op=ALU.mult))
                mp = [ps.alloc(), ps.alloc()]

                def f(en, mp=mp):
                    for j in range(NT):
                        for h in range(4):
                            p0 = (h % 2) * 64
                            i = en.matmul(mp[h // 2].t[p0:p0 + 64, j * 128:(j + 1) * 128], lhsT=vn_tm[j][:, h * 64:(h + 1) * 64],
                                          rhs=wT_sb[:, h * 128:(h + 1) * 128], start=True, stop=True)
                    return i
                op("pe", [("vn", j) for j in range(NT)] + ["wT"], mp, f)
                for pt in range(2):
                    sg = silu_gate(20 + pt)
                    op("dve", [sg, ("gu", pt)], [sg], lambda en, pt=pt, sg=sg: en.tensor_tensor(
                        out=sg.t[:], in0=sg.t[:], in1=gelu_u[:, pt, :], op=ALU.mult))
                    z = rf.alloc()
                    op("dve", [mp[pt], "pcm", "biasB"], [z], lambda en, pt=pt, z=z, mp=mp: en.scalar_tensor_tensor(
                        out=z.t[:, 0:T], in0=mp[pt].t[:], scalar=pc(pt, 3), in1=BiasB[:, pt, :], op0=ALU.mult, op1=ALU.add))
                    op("dve", [z, sg], [("hcat", 2 + pt)], lambda en, pt=pt, z=z, sg=sg: en.tensor_tensor(
                        out=hcat[:, 2 + pt, :], in0=z.t[:, 0:T], in1=sg.t[:], op=ALU.mult))
                cp = [ps.alloc(), ps.alloc()]
                for g in range(4):
                    pt, p0 = g // 2, (g % 2) * 64
                    op("pe", [yp[pt], "poolw"], [cp[pt]], lambda en, pt=pt, p0=p0: en.matmul(
                        cp[pt].t[p0:p0 + 64, :], lhsT=pool_w_sb[p0:p0 + 64, pt, :], rhs=yp[pt].t[p0:p0 + 64, :], start=True, stop=True))
                for pt in range(2):
                    sg = silu_gate(22 + pt)
                    op("dve", [cp[pt], sg, "pcm"], [("hcat", 4 + pt)], lambda en, pt=pt, sg=sg, cp=cp: en.scalar_tensor_tensor(
                        out=hcat[:, 4 + pt, :], in0=cp[pt].t[:], scalar=pc(pt, 5), in1=sg.t[:], op0=ALU.mult, op1=ALU.mult))
                for pt in range(2):
                    acc = accs[pt]
                    bps = proj(2 + pt)
                    op("dve", [bps, acc], [acc], lambda en, acc=acc, bps=bps: en.tensor_tensor(
                        out=acc.t[:, 0:T], in0=bps.t[:], in1=acc.t[:, 0:T], op=ALU.mult))
                    sg = silu_gate(18 + pt)
                    op("dve", [acc, sg], [("hcat", pt)], lambda en, pt=pt, acc=acc, sg=sg: en.tensor_tensor(
                        out=hcat[:, pt, :], in0=acc.t[:, 0:T], in1=sg.t[:], op=ALU.mult))
                if b + 1 < NBLK:
                    emit_xT()
                    if b + 2 < NBLK:
                        load_x(b + 2)
                if b == NBLK - 2 and l + 1 < nl:
                    load_PR(l + 1)
                chk("blk%d" % b)
            outproj(NBLK - 1, [0, 1], False)
            outproj(NBLK - 1, [2, 3], True)
        except StopBuild:
            pass
        for e in tr.engs:
            if tr.cnt[e] > 0:
                tr._wait("sp", (tr.sem[e], tr.cnt[e], e))
        for k, s in tr.dsem.items():
            tr._wait("sp", (s[0], s[1], "dma:" + str(k)))
    return nc


def _consts():
    ident = np.eye(128, dtype=np.float32)
    mask = np.tril(np.ones((128, 128), dtype=np.float32))
    wins = [2, 4, 8, 16]
    invw = np.zeros((128, 2), np.float32)
    invcnt = np.zeros((128, 32), np.float32)
    for pt in range(2):
        for half in range(2):
            w = wins[pt * 2 + half]
            invw[half * 64:(half + 1) * 64, pt] = 1.0 / w
            for t in range(16):
                invcnt[half * 64:(half + 1) * 64, pt * 16 + t] = 1.0 / min(t + 1, w)
    return {"c_ident": ident, "c_mask": mask, "c_invw": invw, "c_invcnt": invcnt}


_NC_CACHE = {}
FUSED = True


def kernel(**inputs):
    x = np.ascontiguousarray(inputs["x"], dtype=np.float32)
    mem = np.ascontiguousarray(inputs["mem"], dtype=np.float32)
    consts = _consts()
    L = inputs["w_in"].shape[0]
    if FUSED:
        if L not in _NC_CACHE:
            _NC_CACHE[L] = build(L)
        nc = _NC_CACHE[L]
        in_maps = []
        for c in range(N_CORES):
            m = {"x": x[c], "mem": mem[c]}
            for p in PARAMS:
                m[p] = np.ascontiguousarray(inputs[p], dtype=np.float32)
            m.update(consts)
            in_maps.append(m)
        res = run_bass_kernel_spmd(nc, in_maps, core_ids=list(range(N_CORES)))
        return np.stack([np.asarray(r["y"]) for r in res.results], axis=0).astype(np.float32)
    if 1 not in _NC_CACHE:
        _NC_CACHE[1] = build(1)
    nc = _NC_CACHE[1]
    cur = x
    for l in range(L):
        in_maps = []
        for c in range(N_CORES):
            m = {"x": np.ascontiguousarray(cur[c]), "mem": mem[c]}
            for p in PARAMS:
                m[p] = np.ascontiguousarray(inputs[p][l:l + 1], dtype=np.float32)
            m.update(consts)
            in_maps.append(m)
        res = run_bass_kernel_spmd(nc, in_maps, core_ids=list(range(N_CORES)))
        cur = np.stack([np.asarray(r["y"]) for r in res.results], axis=0).astype(np.float32)
    return cur
```
